# Optimizing a Trainium2 kernel written in Bass

```python
import jax
import jax.numpy as jnp
from jax import lax
import numpy as np

D_MODEL = 1024
BATCH = 8
SEQ = 2048
DEPTH = 2

DN_HEADS = 4
DN_DK = 128
DN_DV = 128
DN_CONV = 4
DN_CHUNK = 64
DN_WIDTH = DN_HEADS * DN_DV
POOL_WINDOWS = (2, 4, 8, 16)
POOL_GROUP = 128
POOL_WIDTH = POOL_GROUP * len(POOL_WINDOWS)
SA_HEADS = 4
SA_HEAD_DIM = 128
SA_WIDTH = SA_HEADS * SA_HEAD_DIM
IDX_HEADS = 4
IDX_DIM = 64
TOPK_MAX = 256
Q_BLOCK = 128
ROPE_THETA = 10000.0
NORM_EPS = 1e-6
N_BRANCH = 3
BRANCH_WIDTH = 512
SPLIT_SIZES = (DN_WIDTH, DN_WIDTH, DN_WIDTH, DN_WIDTH, DN_HEADS, DN_HEADS,
               POOL_WIDTH, POOL_WIDTH,
               SA_WIDTH, SA_WIDTH, SA_WIDTH, SA_WIDTH, IDX_HEADS * IDX_DIM, IDX_DIM, IDX_HEADS,
               N_BRANCH * D_MODEL)
IN_WIDTH = sum(SPLIT_SIZES)

kernel_name = 'hybrid_deltanet_pool_dsa'


def rms_norm(x, g):
    x32 = x.astype(jnp.float32)
    y = x32 * lax.rsqrt(jnp.mean(x32 * x32, axis=-1, keepdims=True) + NORM_EPS)
    return y.astype(x.dtype) * g


def l2_normalize(x):
    x32 = x.astype(jnp.float32)
    return x32 * lax.rsqrt(jnp.sum(x32 * x32, axis=-1, keepdims=True) + NORM_EPS)


def rope_tables(seq_len, dim):
    inv_freq = ROPE_THETA ** (-jnp.arange(0, dim, 2, dtype=jnp.float32) / dim)
    ang = jnp.arange(seq_len, dtype=jnp.float32)[:, None] * inv_freq[None, :]
    return jnp.cos(ang), jnp.sin(ang)


def apply_rope(x, cos, sin):
    x32 = x.astype(jnp.float32)
    x1, x2 = jnp.split(x32, 2, axis=-1)
    c = cos[None, :, None, :]
    s = sin[None, :, None, :]
    return jnp.concatenate([x1 * c - x2 * s, x2 * c + x1 * s], axis=-1).astype(x.dtype)


def split_columns(proj):
    parts = []
    start = 0
    for size in SPLIT_SIZES:
        parts.append(proj[..., start:start + size])
        start += size
    return parts


def causal_depthwise_conv(x, w):
    k = w.shape[0]
    return lax.conv_general_dilated(x, w[:, None, :], window_strides=(1,), padding=((k - 1, 0),),
                                    dimension_numbers=('NWC', 'WIO', 'NWC'),
                                    feature_group_count=x.shape[-1])


def chunk_gated_delta_rule(q, k, v, beta, g):
    b, t, h, dk = q.shape
    dv = v.shape[-1]
    n = t // DN_CHUNK
    c = DN_CHUNK
    to_chunks4 = lambda a: a.reshape(b, n, c, h, a.shape[-1]).transpose(1, 0, 3, 2, 4)
    to_chunks3 = lambda a: a.reshape(b, n, c, h).transpose(1, 0, 3, 2)
    qc = to_chunks4(q * (dk ** -0.5))
    kc = to_chunks4(k)
    vc = to_chunks4(v)
    bc = to_chunks3(beta)
    gc = jnp.cumsum(to_chunks3(g), axis=-1)
    tril = jnp.tril(jnp.ones((c, c), dtype=bool))
    strict = jnp.tril(jnp.ones((c, c), dtype=bool), -1)
    decay = jnp.exp(jnp.where(tril, gc[..., :, None] - gc[..., None, :], -jnp.inf))
    k_beta = kc * bc[..., None]
    v_beta = vc * bc[..., None]
    lmat = jnp.where(strict, jnp.einsum('nbhid,nbhjd->nbhij', k_beta, kc) * decay, 0.0)
    eye = jnp.eye(c, dtype=jnp.float32)
    tmat = lax.linalg.triangular_solve(eye + lmat, jnp.broadcast_to(eye, lmat.shape),
                                       left_side=True, lower=True, unit_diagonal=True)
    u = jnp.einsum('nbhij,nbhjd->nbhid', tmat, v_beta)
    w = jnp.einsum('nbhij,nbhjd->nbhid', tmat, k_beta * jnp.exp(gc)[..., None])
    attn_intra = jnp.where(tril, jnp.einsum('nbhid,nbhjd->nbhij', qc, kc) * decay, 0.0)

    def step(state, xs):
        q_i, k_i, u_i, w_i, g_i, a_i = xs
        v_new = u_i - jnp.einsum('bhck,bhkv->bhcv', w_i, state)
        o = (jnp.einsum('bhck,bhkv->bhcv', q_i * jnp.exp(g_i)[..., None], state)
             + jnp.einsum('bhij,bhjv->bhiv', a_i, v_new))
        g_last = g_i[..., -1:]
        state = (state * jnp.exp(g_last)[..., None]
                 + jnp.einsum('bhck,bhcv->bhkv', k_i * jnp.exp(g_last - g_i)[..., None], v_new))
        return state, o

    state0 = jnp.zeros((b, h, dk, dv), dtype=jnp.float32)
    _, o = lax.scan(step, state0, (qc, kc, u, w, gc, attn_intra))
    return o.transpose(1, 0, 3, 2, 4).reshape(b, t, h, dv)


def deltanet_branch(qa, ka, va, za, ba, aa, conv_w, a_log, dt_bias, o_norm):
    b, t, _ = qa.shape
    qkv = jax.nn.silu(causal_depthwise_conv(jnp.concatenate([qa, ka, va], axis=-1), conv_w))
    q, k, v = jnp.split(qkv, 3, axis=-1)
    q = l2_normalize(q.reshape(b, t, DN_HEADS, DN_DK))
    k = l2_normalize(k.reshape(b, t, DN_HEADS, DN_DK))
    v = v.reshape(b, t, DN_HEADS, DN_DV).astype(jnp.float32)
    beta = jax.nn.sigmoid(ba.astype(jnp.float32))
    g = -jnp.exp(a_log.astype(jnp.float32)) * jax.nn.softplus(aa.astype(jnp.float32) + dt_bias.astype(jnp.float32))
    o = chunk_gated_delta_rule(q, k, v, beta, g)
    o = rms_norm(o, o_norm) * jax.nn.silu(za.reshape(b, t, DN_HEADS, DN_DV))
    return o.reshape(b, t, DN_WIDTH)


def pooling_branch(u, z, pool_w, pool_scale):
    b, t, _ = u.shape
    u32 = u.astype(jnp.float32)
    cs = jnp.cumsum(u32, axis=1)
    pos_count = jnp.arange(1, t + 1, dtype=jnp.int32)
    groups = []
    for gi, win in enumerate(POOL_WINDOWS):
        sl = slice(gi * POOL_GROUP, (gi + 1) * POOL_GROUP)
        cs_g = cs[..., sl]
        lagged = jnp.pad(cs_g, ((0, 0), (win, 0), (0, 0)))[:, :t]
        count = jnp.minimum(pos_count, win).astype(jnp.float32)[None, :, None]
        groups.append((cs_g - lagged) / count - u32[..., sl])
    pooled = jnp.stack(groups, axis=2)
    mixed = jnp.einsum('btgc,gcd->btgd', pooled.astype(pool_w.dtype), pool_w).reshape(b, t, POOL_WIDTH)
    return mixed * pool_scale * jax.nn.silu(z)


def sparse_attention_branch(qc, kc, vc, zc, iq, ik, iw, q_norm, k_norm, cos_a, sin_a, cos_i, sin_i):
    b, t, _ = qc.shape
    q = apply_rope(rms_norm(qc.reshape(b, t, SA_HEADS, SA_HEAD_DIM), q_norm), cos_a, sin_a)
    k = apply_rope(rms_norm(kc.reshape(b, t, SA_HEADS, SA_HEAD_DIM), k_norm), cos_a, sin_a)
    v = vc.reshape(b, t, SA_HEADS, SA_HEAD_DIM)
    qi = apply_rope(iq.reshape(b, t, IDX_HEADS, IDX_DIM), cos_i, sin_i)
    ki = apply_rope(ik.reshape(b, t, 1, IDX_DIM), cos_i, sin_i)[:, :, 0].astype(jnp.float32)
    wi = iw.astype(jnp.float32) * (IDX_HEADS ** -0.5 * IDX_DIM ** -0.5)
    topk = min(TOPK_MAX, t // 4)
    nb = t // Q_BLOCK
    to_blocks = lambda a: a.reshape((b, nb, Q_BLOCK) + a.shape[2:]).swapaxes(0, 1)
    key_pos = jnp.arange(t, dtype=jnp.int32)
    gather_rows = jax.vmap(lambda src, idx: src[idx])

    def attend(xs):
        q_b, qi_b, wi_b, start = xs
        q_pos = start + jnp.arange(Q_BLOCK, dtype=jnp.int32)
        causal = key_pos[None, :] <= q_pos[:, None]
        dots = jnp.einsum('bqhd,bsd->bqhs', qi_b.astype(jnp.float32), ki)
        score = jnp.einsum('bqh,bqhs->bqs', wi_b, jax.nn.relu(dots))
        score = jnp.where(causal[None], score, -jnp.inf)
        _, idx = lax.top_k(score, topk)
        k_sel = gather_rows(k, idx)
        v_sel = gather_rows(v, idx)
        logits = jnp.einsum('bqhd,bqkhd->bhqk', q_b.astype(jnp.float32), k_sel.astype(jnp.float32)) * (SA_HEAD_DIM ** -0.5)
        valid = idx <= q_pos[None, :, None]
        logits = jnp.where(valid[:, None], logits, -jnp.inf)
        p = jax.nn.softmax(logits, axis=-1)
        return jnp.einsum('bhqk,bqkhd->bqhd', p.astype(v_sel.dtype), v_sel)

    starts = jnp.arange(nb, dtype=jnp.int32) * Q_BLOCK
    o = lax.map(attend, (to_blocks(q), to_blocks(qi), to_blocks(wi), starts))
    o = o.swapaxes(0, 1).reshape(b, t, SA_WIDTH)
    return o * jax.nn.silu(zc)


def hybrid_layer(x, norm_g, w_in, gate_b, conv_w, a_log, dt_bias, dn_onorm, pool_w, pool_scale,
                 q_norm, k_norm, w_branch, w_out, cos_a, sin_a, cos_i, sin_i):
    b, t, _ = x.shape
    h = rms_norm(x, norm_g)
    proj = jnp.einsum('btd,de->bte', h, w_in)
    (qa, ka, va, za, ba, aa, ub, zb, qc, kc, vc, zc, iq, ik, iw, gates) = split_columns(proj)
    ya = deltanet_branch(qa, ka, va, za, ba, aa, conv_w, a_log, dt_bias, dn_onorm)
    yb = pooling_branch(ub, zb, pool_w, pool_scale)
    yc = sparse_attention_branch(qc, kc, vc, zc, iq, ik, iw, q_norm, k_norm, cos_a, sin_a, cos_i, sin_i)
    branches = jnp.stack([ya, yb, yc], axis=2).astype(w_branch.dtype)
    projected = jnp.einsum('btnc,ncd->btnd', branches, w_branch)
    gate = jax.nn.sigmoid(gates.reshape(b, t, N_BRANCH, D_MODEL).astype(jnp.float32) + gate_b.astype(jnp.float32))
    merged = jnp.sum(gate * projected, axis=2).astype(w_out.dtype)
    return x + jnp.einsum('btd,de->bte', merged, w_out).astype(x.dtype)


def setup_inputs(seed: int = 0) -> dict:
    key = jax.random.key(seed)
    ks = jax.random.split(key, 16)
    f32 = jnp.float32
    x = jax.random.normal(ks[0], (BATCH, SEQ, D_MODEL), f32)
    norm_g = 1.0 + 0.02 * jax.random.normal(ks[1], (DEPTH, D_MODEL), f32)
    w_in = jax.random.normal(ks[2], (DEPTH, D_MODEL, IN_WIDTH), f32) * D_MODEL ** -0.5
    gate_b = 0.02 * jax.random.normal(ks[3], (DEPTH, N_BRANCH, D_MODEL), f32)
    conv_w = jax.random.normal(ks[4], (DEPTH, DN_CONV, 3 * DN_WIDTH), f32) * DN_CONV ** -0.5
    a_log = jnp.log(jax.random.uniform(ks[5], (DEPTH, DN_HEADS), f32, minval=1.0, maxval=16.0))
    dt = jnp.exp(jax.random.uniform(ks[6], (DEPTH, DN_HEADS), f32, minval=float(np.log(1e-3)), maxval=float(np.log(1e-1))))
    dt_bias = dt + jnp.log(-jnp.expm1(-dt))
    dn_onorm = 1.0 + 0.02 * jax.random.normal(ks[7], (DEPTH, DN_DV), f32)
    pool_w = jax.random.normal(ks[8], (DEPTH, len(POOL_WINDOWS), POOL_GROUP, POOL_GROUP), f32) * POOL_GROUP ** -0.5
    pool_scale = 1.0 + 0.1 * jax.random.normal(ks[9], (DEPTH, POOL_WIDTH), f32)
    q_norm = 1.0 + 0.02 * jax.random.normal(ks[10], (DEPTH, SA_HEAD_DIM), f32)
    k_norm = 1.0 + 0.02 * jax.random.normal(ks[11], (DEPTH, SA_HEAD_DIM), f32)
    w_branch = jax.random.normal(ks[12], (DEPTH, N_BRANCH, BRANCH_WIDTH, D_MODEL), f32) * BRANCH_WIDTH ** -0.5
    w_out = jax.random.normal(ks[13], (DEPTH, D_MODEL, D_MODEL), f32) * D_MODEL ** -0.5
    return {'x': x, 'norm_g': norm_g, 'w_in': w_in, 'gate_b': gate_b, 'conv_w': conv_w,
            'a_log': a_log, 'dt_bias': dt_bias, 'dn_onorm': dn_onorm, 'pool_w': pool_w,
            'pool_scale': pool_scale, 'q_norm': q_norm, 'k_norm': k_norm,
            'w_branch': w_branch, 'w_out': w_out}


def reference(x, norm_g, w_in, gate_b, conv_w, a_log, dt_bias, dn_onorm, pool_w, pool_scale,
              q_norm, k_norm, w_branch, w_out):
    t = x.shape[1]
    cos_a, sin_a = rope_tables(t, SA_HEAD_DIM)
    cos_i, sin_i = rope_tables(t, IDX_DIM)
    for layer in range(DEPTH):
        x = hybrid_layer(x, norm_g[layer], w_in[layer], gate_b[layer], conv_w[layer], a_log[layer],
                         dt_bias[layer], dn_onorm[layer], pool_w[layer], pool_scale[layer],
                         q_norm[layer], k_norm[layer], w_branch[layer], w_out[layer],
                         cos_a, sin_a, cos_i, sin_i)
    return x
```

```python
from contextlib import ExitStack
import os
import numpy as np
import ml_dtypes
import concourse.bass as bass
import concourse.mybir as mybir
from concourse.bass_utils import run_bass_kernel_spmd

F32 = mybir.dt.float32
BF16 = mybir.dt.bfloat16
ALU = mybir.AluOpType
AF = mybir.ActivationFunctionType
AX = mybir.AxisListType

STOP = int(os.environ.get('STOP', 0))
T = 2048
D = 1024
NT = 16
INW = 8524
EPS = 1e-6
TOPK = 256
NBIS = 18
O_QA, O_KA, O_VA, O_ZA, O_BA = 0, 512, 1024, 1536, 2048
O_UB, O_ZB = 2056, 2568
O_QC, O_KC, O_VC, O_ZC, O_IQ = 3080, 3592, 4104, 4616, 5128
O_G = 5452
PP_CONV, PP_GB, PP_PS, PP_ALOG, PP_DTB, PP_ON, PP_QN, PP_KN, PP_G = 0, 48, 72, 76, 80, 84, 212, 340, 468
PP_W = 468 + 1024
CF_ID, CF_TRIU, CF_ONES, CF_MASKA, CF_NEG = 0, 128, 256, 384, 512
CB_ID, CB_ONES, CB_STRICT, CB_MT = 0, 128, 256, 384


class Buf:
    __slots__ = ("name", "t", "lw", "rd", "dsem", "excl")

    def __init__(self, name, t=None, excl=False):
        self.name = name
        self.t = t
        self.excl = excl
        self.lw = None
        self.rd = {}
        self.dsem = None

    def __getitem__(self, k):
        return self.t[k]


class Prog:
    COMPUTE = ("pe", "act", "dve", "pool")

    def __init__(self, nc, es, same_sync=("act", "dve", "pool")):
        self.nc = nc
        self.es = es
        self.eng = {"pe": nc.tensor, "act": nc.scalar, "dve": nc.vector, "pool": nc.gpsimd, "sp": nc.sync}
        self.sem = {}
        self.cnt = {}
        self.seen = {}
        self.same_sync = set(same_sync)
        for e in self.COMPUTE:
            self.sem[e] = es.enter_context(nc.semaphore("s_" + e))
            self.cnt[e] = 0
        self.n_dsem = 0
        self.n_ops = 0
        self.n_waits = 0

    def sb(self, name, shape, dt, stack=None):
        self.n_names = getattr(self, "n_names", 0) + 1
        name = "sb%d_%s" % (self.n_names, name)
        t = (stack or self.es).enter_context(self.nc.sbuf_tensor(name, shape, dt))
        return Buf(name, t)

    def ps(self, name, shape, dt, stack=None):
        t = (stack or self.es).enter_context(self.nc.psum_tensor(name, shape, dt))
        return Buf(name, t, excl=True)

    def dsem_for(self, buf):
        if buf.dsem is None:
            key = "d%d" % self.n_dsem
            self.n_dsem += 1
            self.sem[key] = self.es.enter_context(self.nc.semaphore("s_" + key))
            self.cnt[key] = 0
            buf.dsem = key
        return buf.dsem

    def _waits(self, issuer, selfkey, reads, writes):
        need = {}
        for b in reads:
            if b.lw is not None:
                e, n = b.lw
                if need.get(e, 0) < n:
                    need[e] = n
            if b.excl:
                for e, n in b.rd.items():
                    if e != selfkey and need.get(e, 0) < n:
                        need[e] = n
        for b in writes:
            if b.lw is not None:
                e, n = b.lw
                if need.get(e, 0) < n:
                    need[e] = n
            for e, n in b.rd.items():
                if need.get(e, 0) < n:
                    need[e] = n
        seen = self.seen.setdefault(issuer, {})
        for e, n in need.items():
            if e == selfkey and e not in self.same_sync:
                continue
            if seen.get(e, 0) < n:
                self.eng[issuer].wait_ge(self.sem[e], n)
                seen[e] = n
                self.n_waits += 1

    def _commit(self, key, n, reads, writes):
        for b in reads:
            if b.rd.get(key, 0) < n:
                b.rd[key] = n
        for b in writes:
            b.lw = (key, n)
            b.rd = {}

    def op(self, eng, emit, reads=(), writes=()):
        self._waits(eng, eng, reads, writes)
        ins = emit(self.eng[eng])
        self.cnt[eng] += 1
        ins.then_inc(self.sem[eng], 1)
        self._commit(eng, self.cnt[eng], reads, writes)
        self.n_ops += 1
        return ins

    def dma(self, queue, out, in_, reads=(), writes=(), sembuf=None, **kw):
        sbf = sembuf if sembuf is not None else writes[0]
        key = self.dsem_for(sbf)
        self._waits(queue, None, reads, writes)
        ins = self.eng[queue].dma_start(out=out, in_=in_, **kw)
        self.cnt[key] += 16
        ins.then_inc(self.sem[key], 16)
        self._commit(key, self.cnt[key], reads, writes)
        self.n_ops += 1
        return ins

    def wait_all(self, issuer, bufs):
        self._waits(issuer, issuer, bufs, ())

    def barrier(self):
        for e in self.COMPUTE:
            seen = self.seen.setdefault(e, {})
            for e2 in self.COMPUTE:
                if e2 == e:
                    continue
                n = self.cnt[e2]
                if n > 0 and seen.get(e2, 0) < n:
                    self.eng[e].wait_ge(self.sem[e2], n)
                    seen[e2] = n


def ts(i, n=128):
    return slice(i * n, (i + 1) * n)


def build(n_layers=2, phases="NA123BCM", debug=False):
    nc = bass.Bass("TRN2", target_bir_lowering=False)
    dt_in = lambda name, shape, dt=F32: nc.dram_tensor(name, shape, dt, kind="ExternalInput").ap()
    x_in = dt_in("x", [T, D])
    w_in = dt_in("w_in", [2, D, INW])
    w_br = dt_in("w_branch", [2, 3, 512, D])
    w_out = dt_in("w_out", [2, D, D])
    pool_w = dt_in("pool_w", [2, 4, 128, 128])
    pp_d = dt_in("pp", [2, 128, PP_W])
    cf_d = dt_in("cf", [128, 640])
    cb_d = dt_in("cb", [128, 1920], BF16)
    ropeA_d = dt_in("ropeA", [128, 16, 256])
    ropeI_d = dt_in("ropeI", [128, 16, 128])
    out_d = nc.dram_tensor("out", [T, D], F32, kind="ExternalOutput").ap()
    skind = "ExternalOutput" if debug else "Internal"
    xs_d = nc.dram_tensor("xs", [T, D], F32, kind="Internal").ap()
    yT_d = nc.dram_tensor("yT", [3, 512, T], BF16, kind=skind).ap()

    with ExitStack() as es:
        P = Prog(nc, es)
        op, dma = P.op, P.dma
        cF = P.sb("cF", [128, 640], F32)
        cB = P.sb("cB", [128, 1920], BF16)
        hT = P.sb("hT", [128, 8, T], BF16)
        pp = P.sb("pp", [128, PP_W], F32)
        banks = [P.ps("bk%d" % i, [128, 512], F32) for i in range(8)]
        xs_b = Buf("xs_d")
        yT_b = [Buf("yT%d" % i) for i in range(3)]
        out_b = Buf("out_d")
        state = {"tb": 0, "pool": list(range(8))}

        def tb():
            lst = state["pool"]
            b = banks[lst[state["tb"] % len(lst)]]
            state["tb"] += 1
            return b

        def bfv(bank):
            return bank.t[:].bitcast(BF16)

        dma("sp", cF[:], cf_d, writes=[cF])
        dma("sp", cB[:], cb_d, writes=[cB])
        idF = cF[:, CF_ID:CF_ID + 128]
        triuF = cF[:, CF_TRIU:CF_TRIU + 128]
        onesF = cF[:, CF_ONES:CF_ONES + 128]
        maskA = cF[:, CF_MASKA:CF_MASKA + 128]
        negM = cF[:, CF_NEG:CF_NEG + 128]
        idB = cB[:, CB_ID:CB_ID + 128]
        onesB = cB[:, CB_ONES:CB_ONES + 128]
        strictB = cB[:, CB_STRICT:CB_STRICT + 128]

        def mm(bank, osl, lhsT, rhs, reads, start=True, stop=True):
            op("pe", lambda e: e.matmul(bank.t[:, osl], lhsT=lhsT, rhs=rhs, start=start, stop=stop),
               reads=reads, writes=[bank])

        def tr(bank, osl, in_, reads):
            op("pe", lambda e: e.transpose(bfv(bank)[:, osl], in_, idB), reads=list(reads) + [cB], writes=[bank])

        for l in range(n_layers):
            xsrc, xsrc_b = (x_in, None) if l == 0 else (xs_d, xs_b)
            xdst, xdst_b = (xs_d, xs_b) if l < n_layers - 1 else (out_d, out_b)
            dma("sp", pp[:], pp_d[l], writes=[pp])
            wv = w_in[l].rearrange("(k p) e -> p k e", p=128)

            with ExitStack() as ph:
                xt = [P.sb("xt%d" % i, [128, D], F32, ph) for i in range(2)]
                junk = P.sb("njunk", [128, D], BF16, ph)
                hb = [P.sb("hb%d" % i, [128, D], BF16, ph) for i in range(2)]
                ssq = P.sb("ssq", [128, 2 * NT], F32, ph)
                op("dve", lambda e: e.memset(ssq[:], 0.0), writes=[ssq])
                for n in range(NT):
                    xb_ = xt[n % 2]
                    dma("sp", xb_[:], xsrc[ts(n), :], reads=[xsrc_b] if xsrc_b else [], writes=[xb_])
                    op("act", lambda e: e.activation(out=junk[:], in_=xb_[:], func=AF.Square,
                                                     accum_out=ssq[:, n:n + 1]), reads=[xb_], writes=[junk, ssq])
                    op("act", lambda e: e.activation(out=ssq[:, NT + n:NT + n + 1], in_=ssq[:, n:n + 1], func=AF.Sqrt,
                                                     bias=EPS, scale=1.0 / D), reads=[ssq], writes=[ssq])
                    op("dve", lambda e: e.reciprocal(out=ssq[:, NT + n:NT + n + 1], in_=ssq[:, NT + n:NT + n + 1]),
                       reads=[ssq], writes=[ssq])
                    hb_ = hb[n % 2]
                    op("dve", lambda e: e.scalar_tensor_tensor(out=hb_[:], in0=xb_[:], scalar=ssq[:, NT + n:NT + n + 1],
                                                               in1=pp[:, PP_G:PP_G + D], op0=ALU.mult, op1=ALU.mult),
                       reads=[xb_, ssq, pp], writes=[hb_])
                    bk = tb()
                    for k in range(8):
                        tr(bk, ts(k), hb_[:, ts(k)], [hb_])
                    op("act", lambda e: e.copy(out=hT[:, :, ts(n)], in_=bfv(bk).rearrange("p (k t) -> p k t", k=8)),
                       reads=[bk], writes=[hT])
            P.barrier()

            def load_w(dst, c0, ncols, src=None):
                dma("pool", dst[:, :, 0:ncols], wv[:, :, c0:c0 + ncols], writes=[dst])

            if "A" in phases:
                with ExitStack() as ph:
                    qkvT = P.sb("qkvT", [128, 12, T], BF16, ph)
                    zA = P.sb("zA", [128, NT, 512], BF16, ph)
                    raw8 = P.sb("raw8", [128, NT, 8], F32, ph)
                    betaT = P.sb("betaT", [128, NT, 4], F32, ph)
                    gT = P.sb("gT", [128, NT, 4], F32, ph)
                    with ExitStack() as ph1:
                        wsl = [P.sb("wA%d" % i, [128, 8, 128], BF16, ph1) for i in range(2)]
                        cbuf = [P.sb("cbuf%d" % i, [128, 3 + T], F32, ph1) for i in range(2)]
                        acc = P.sb("acc", [128, T], F32, ph1)
                        sl = P.sb("sl", [128, T], F32, ph1)
                        sq = P.sb("sq", [128, T], BF16, ph1)
                        rn = [P.sb("rn%d" % i, [128, 512], F32, ph1) for i in range(2)]
                        for c_ in cbuf:
                            op("pool", lambda e: e.memset(c_[:, 0:3], 0.0), writes=[c_])
                        for cc in range(12 if '1' in phases else 0):
                            wb = wsl[cc % 2]
                            load_w(wb, cc * 128, 128)
                            cb_ = cbuf[cc % 2]
                            for tg in range(4):
                                bk = tb()
                                for k in range(8):
                                    mm(bk, slice(0, 512), wb[:, k, :], hT[:, k, ts(tg, 512)], [wb, hT], k == 0, k == 7)
                                op("act", lambda e: e.copy(out=cb_[:, 3 + tg * 512:3 + (tg + 1) * 512], in_=bk[:]),
                                   reads=[bk], writes=[cb_])
                            cw = lambda j: pp[:, PP_CONV + cc * 4 + j:PP_CONV + cc * 4 + j + 1]
                            op("pool", lambda e: e.tensor_scalar(out=acc[:], in0=cb_[:, 0:T], scalar1=cw(0), scalar2=None,
                                                                 op0=ALU.mult), reads=[cb_, pp], writes=[acc])
                            op("dve", lambda e: e.scalar_tensor_tensor(out=acc[:], in0=cb_[:, 1:1 + T], scalar=cw(1), in1=acc[:],
                                                                        op0=ALU.mult, op1=ALU.add), reads=[cb_, pp, acc], writes=[acc])
                            op("dve", lambda e: e.scalar_tensor_tensor(out=acc[:], in0=cb_[:, 2:2 + T], scalar=cw(2), in1=acc[:],
                                                                       op0=ALU.mult, op1=ALU.add), reads=[cb_, pp, acc], writes=[acc])
                            op("dve", lambda e: e.scalar_tensor_tensor(out=acc[:], in0=cb_[:, 3:3 + T], scalar=cw(3), in1=acc[:],
                                                                       op0=ALU.mult, op1=ALU.add), reads=[cb_, pp, acc], writes=[acc])
                            if cc >= 8:
                                op("act", lambda e: e.activation(out=qkvT[:, cc, :], in_=acc[:], func=AF.Silu),
                                   reads=[acc], writes=[qkvT])
                            else:
                                op("act", lambda e: e.activation(out=sl[:], in_=acc[:], func=AF.Silu), reads=[acc], writes=[sl])
                                op("pool", lambda e: e.tensor_tensor(out=sq[:], in0=sl[:], in1=sl[:], op=ALU.mult),
                                   reads=[sl], writes=[sq])
                                for tg in range(4):
                                    bk = tb()
                                    mm(bk, slice(0, 512), onesB, sq[:, ts(tg, 512)], [cB, sq])
                                    rn_ = rn[tg % 2]
                                    op("act", lambda e: e.activation(out=rn_[:], in_=bk[:], func=AF.Sqrt, bias=EPS, scale=1.0),
                                       reads=[bk], writes=[rn_])
                                    op("dve", lambda e: e.reciprocal(out=rn_[:], in_=rn_[:]), reads=[rn_], writes=[rn_])
                                    sc = 128.0 ** -0.5 if cc < 4 else 1.0
                                    op("dve", lambda e: e.scalar_tensor_tensor(out=qkvT[:, cc, ts(tg, 512)], in0=sl[:, ts(tg, 512)],
                                                                               scalar=sc, in1=rn_[:], op0=ALU.mult, op1=ALU.mult),
                                       reads=[sl, rn_], writes=[qkvT])
                        wz = P.sb("wz", [128, 8, 520], BF16, ph1)
                        load_w(wz, O_ZA, 520)
                        zt = [P.sb("zt%d" % i, [128, 512], BF16, ph1) for i in range(2)]
                        onB = pp[:, PP_ON:PP_ON + 128].unsqueeze(1).to_broadcast([128, 4, 128])
                        for n in range(NT if '2' in phases else 0):
                            bk = tb()
                            bk2 = tb()
                            for k in range(8):
                                mm(bk, slice(0, 512), hT[:, k, ts(n)], wz[:, k, 0:512], [wz, hT], k == 0, k == 7)
                            for k in range(8):
                                mm(bk2, slice(0, 8), hT[:, k, ts(n)], wz[:, k, 512:520], [wz, hT], k == 0, k == 7)
                            zt_ = zt[n % 2]
                            op("act", lambda e: e.activation(out=zt_[:], in_=bk[:], func=AF.Silu), reads=[bk], writes=[zt_])
                            op("pool", lambda e: e.tensor_tensor(out=zA[:, n, :].rearrange("p (h d) -> p h d", h=4),
                                                                 in0=zt_[:].rearrange("p (h d) -> p h d", h=4), in1=onB, op=ALU.mult),
                               reads=[zt_, pp], writes=[zA])
                            op("dve", lambda e: e.tensor_copy(out=raw8[:, n, :], in_=bk2[:, 0:8]), reads=[bk2], writes=[raw8])
                        op("act", lambda e: e.activation(out=betaT[:], in_=raw8[:, :, 0:4], func=AF.Sigmoid), reads=[raw8], writes=[betaT])
                        dtb = pp[:, PP_DTB:PP_DTB + 4].unsqueeze(1).to_broadcast([128, NT, 4])
                        op("dve", lambda e: e.tensor_tensor(out=gT[:], in0=raw8[:, :, 4:8], in1=dtb, op=ALU.add), reads=[raw8, pp], writes=[gT])
                        op("act", lambda e: e.activation(out=gT[:], in_=gT[:], func=AF.Exp), reads=[gT], writes=[gT])
                        op("act", lambda e: e.activation(out=gT[:], in_=gT[:], func=AF.Ln, bias=1.0, scale=1.0), reads=[gT], writes=[gT])
                        negA = P.sb("negA", [128, 4], F32, ph1)
                        op("act", lambda e: e.activation(out=negA[:], in_=pp[:, PP_ALOG:PP_ALOG + 4], func=AF.Exp), reads=[pp], writes=[negA])
                        op("dve", lambda e: e.tensor_scalar(out=negA[:], in0=negA[:], scalar1=-1.0, scalar2=None, op0=ALU.mult),
                           reads=[negA], writes=[negA])
                        op("dve", lambda e: e.tensor_tensor(out=gT[:], in0=gT[:], in1=negA[:].unsqueeze(1).to_broadcast([128, NT, 4]),
                                                            op=ALU.mult), reads=[gT, negA], writes=[gT])
                        P.barrier()
                    with ExitStack() as ph2:
                        Sf = [P.sb("Sf%d" % h, [128, 128], F32, ph2) for h in range(4)]
                        Sb = [P.sb("Sb%d" % h, [128, 128], BF16, ph2) for h in range(4)]
                        for h in range(4):
                            op("pool", lambda e: e.memset(Sf[h][:], 0.0), writes=[Sf[h]])
                            op("pool", lambda e: e.memset(Sb[h][:], 0.0), writes=[Sb[h]])
                        NB_ = 2
                        mk = lambda nm, shp, dt: [[P.sb("%s_%d_%d" % (nm, i, h), shp, dt, ph2) for h in range(4)] for i in range(NB_)]
                        g12 = [P.sb("g12_%d" % i, [128, 12], F32, ph2) for i in range(NB_)]
                        e12 = [P.sb("e12_%d" % i, [128, 12], F32, ph2) for i in range(NB_)]
                        be4 = [P.sb("be4_%d" % i, [128, 4], F32, ph2) for i in range(NB_)]
                        rhsg = mk("rhsg", [128, 128], F32)
                        dec = mk("dec", [128, 128], F32)
                        decs = mk("decs", [128, 128], F32)
                        Lb = mk("Lb", [128, 128], BF16)
                        At = mk("At", [128, 128], BF16)
                        XA = mk("XA", [128, 256], BF16)
                        kbg = mk("kbg", [128, 128], BF16)
                        kd = mk("kd", [128, 128], BF16)
                        vb = mk("vb", [128, 128], BF16)
                        XY = [mk("XY%d" % j, [128, 256], BF16) for j in range(2)]
                        Pm = [mk("Pm%d" % j, [128, 128], BF16) for j in range(2)]
                        wTs = mk("wTs", [128, 128], BF16)
                        us = mk("us", [128, 128], F32)
                        vn = mk("vn", [128, 128], BF16)
                        otmp = mk("otmp", [128, 128], F32)
                        o_all = [P.sb("o_all%d" % i, [128, 4, 128], F32, ph2) for i in range(NB_)]
                        o_sq = [P.sb("o_sq%d" % i, [128, 4, 128], F32, ph2) for i in range(NB_)]
                        r4 = [P.sb("r4_%d" % i, [128, 4], F32, ph2) for i in range(NB_)]
                        ya = [P.sb("ya%d" % i, [128, 512], BF16, ph2) for i in range(NB_)]
                        yaT = [P.sb("yaT%d" % i, [128, 4, 128], BF16, ph2) for i in range(NB_)]
                        for n in range(int(os.environ.get('A3N', NT)) if '3' in phases else 0):
                            i2 = n % NB_
                            bk = tb()
                            mm(bk, slice(0, 4), triuF, gT[:, n, :], [cF, gT])
                            mm(bk, slice(4, 8), onesF, gT[:, n, :], [cF, gT])
                            g_, e_ = g12[i2], e12[i2]
                            op("dve", lambda e: e.tensor_copy(out=g_[:, 0:4], in_=bk[:, 0:4]), reads=[bk], writes=[g_])
                            op("dve", lambda e: e.tensor_copy(out=g_[:, 8:12], in_=bk[:, 4:8]), reads=[bk], writes=[g_])
                            op("dve", lambda e: e.tensor_tensor(out=g_[:, 4:8], in0=g_[:, 8:12], in1=g_[:, 0:4], op=ALU.subtract),
                               reads=[g_], writes=[g_])
                            op("act", lambda e: e.activation(out=e_[:], in_=g_[:], func=AF.Exp), reads=[g_], writes=[e_])
                            b4 = be4[i2]
                            op("dve", lambda e: e.tensor_tensor(out=b4[:], in0=betaT[:, n, :], in1=e_[:, 0:4], op=ALU.mult),
                               reads=[betaT, e_], writes=[b4])
                            for h in range(4):
                                kTc = qkvT[:, 4 + h, ts(n)]
                                qTc = qkvT[:, h, ts(n)]
                                vTc = qkvT[:, 8 + h, ts(n)]
                                rg, dc, dcs = rhsg[i2][h], dec[i2][h], decs[i2][h]
                                op("pool", lambda e: e.tensor_scalar(out=rg[:], in0=triuF, scalar1=gT[:, n, h:h + 1], scalar2=None,
                                                                     op0=ALU.mult), reads=[cF, gT], writes=[rg])
                                if STOP == 1: break
                                bD = tb()
                                mm(bD, slice(0, 128), onesF, rg[:], [cF, rg], True, False)
                                mm(bD, slice(0, 128), idF, maskA, [cF], False, True)
                                op("act", lambda e: e.activation(out=dc[:], in_=bD[:, 0:128], func=AF.Exp, scale=-1.0,
                                                                 bias=g_[:, h:h + 1]), reads=[bD, g_], writes=[dc])
                                op("pool", lambda e: e.tensor_tensor(out=dcs[:], in0=dc[:], in1=strictB, op=ALU.mult),
                                   reads=[dc, cB], writes=[dcs])
                                if STOP == 2: break
                                bK = tb()
                                mm(bK, slice(0, 128), kTc, kTc, [qkvT])
                                mm(bK, slice(128, 256), qTc, kTc, [qkvT])
                                L_, A_ = Lb[i2][h], At[i2][h]
                                op("dve", lambda e: e.scalar_tensor_tensor(out=L_[:], in0=bK[:, 0:128], scalar=betaT[:, n, h:h + 1],
                                                                           in1=dcs[:], op0=ALU.mult, op1=ALU.mult),
                                   reads=[bK, betaT, dcs], writes=[L_])
                                op("dve", lambda e: e.tensor_tensor(out=A_[:], in0=bK[:, 128:256], in1=dc[:], op=ALU.mult),
                                   reads=[bK, dc], writes=[A_])
                                if STOP == 3: break
                                bT = tb()
                                tr(bT, slice(0, 128), L_[:], [L_])
                                tr(bT, slice(128, 256), A_[:], [A_])
                                tr(bT, slice(256, 384), kTc, [qkvT])
                                tr(bT, slice(384, 512), vTc, [qkvT])
                                xa = XA[i2][h]
                                op("act", lambda e: e.copy(out=xa[:], in_=bfv(bT)[:, 0:256]), reads=[bT], writes=[xa])
                                kbg_, kd_, vb_ = kbg[i2][h], kd[i2][h], vb[i2][h]
                                op("act", lambda e: e.activation(out=kbg_[:], in_=bfv(bT)[:, 256:384], func=AF.Copy,
                                                                 scale=b4[:, h:h + 1]), reads=[bT, b4], writes=[kbg_])
                                op("dve", lambda e: e.tensor_scalar(out=kd_[:], in0=bfv(bT)[:, 256:384], scalar1=e_[:, 4 + h:5 + h],
                                                                    scalar2=None, op0=ALU.mult), reads=[bT, e_], writes=[kd_])
                                op("dve", lambda e: e.tensor_scalar(out=vb_[:], in0=bfv(bT)[:, 384:512], scalar1=betaT[:, n, h:h + 1],
                                                                    scalar2=None, op0=ALU.mult), reads=[bT, betaT], writes=[vb_])
                                if STOP == 4: break
                                p_ = Pm[0][i2][h]
                                op("dve", lambda e: e.tensor_tensor(out=p_[:], in0=idB, in1=xa[:, 0:128], op=ALU.subtract),
                                   reads=[cB, xa], writes=[p_])
                                Xc, Yc, Xb, Yb = xa[:, 0:128], L_[:], xa, L_
                                for m in range(int(os.environ.get('NEU', 6))):
                                    xy = XY[m % 2][i2][h]
                                    bN = tb()
                                    mm(bN, slice(0, 128), Yc, Xc, [Xb, Yb])
                                    mm(bN, slice(128, 256), Xc, Yc, [Xb, Yb])
                                    op("act" if m % 2 == 0 else "dve",
                                       (lambda e: e.copy(out=xy[:], in_=bN[:, 0:256])) if m % 2 == 0 else
                                       (lambda e: e.tensor_copy(out=xy[:], in_=bN[:, 0:256])), reads=[bN], writes=[xy])
                                    Xc, Yc, Xb, Yb = xy[:, 0:128], xy[:, 128:256], xy, xy
                                    pn = Pm[(m + 1) % 2][i2][h]
                                    bP = tb()
                                    mm(bP, slice(0, 128), idB, p_[:], [cB, p_], True, False)
                                    mm(bP, slice(0, 128), Yc, p_[:], [xy, p_], False, True)
                                    op("dve" if m % 2 == 0 else "act",
                                       (lambda e: e.tensor_copy(out=pn[:], in_=bP[:, 0:128])) if m % 2 == 0 else
                                       (lambda e: e.copy(out=pn[:], in_=bP[:, 0:128])), reads=[bP], writes=[pn])
                                    p_ = pn
                                TT = p_
                                if STOP == 5: break
                                bW = tb()
                                mm(bW, slice(0, 128), kbg_[:], TT[:], [kbg_, TT])
                                mm(bW, slice(128, 256), TT[:], vb_[:], [TT, vb_])
                                w_, u_ = wTs[i2][h], us[i2][h]
                                op("act", lambda e: e.copy(out=w_[:], in_=bW[:, 0:128]), reads=[bW], writes=[w_])
                                op("dve", lambda e: e.tensor_copy(out=u_[:], in_=bW[:, 128:256]), reads=[bW], writes=[u_])
                                if STOP == 6: break
                                b1 = tb()
                                mm(b1, slice(0, 128), w_[:], Sb[h][:], [w_, Sb[h]])
                                vn_ = vn[i2][h]
                                op("dve", lambda e: e.tensor_tensor(out=vn_[:], in0=u_[:], in1=b1[:, 0:128], op=ALU.subtract),
                                   reads=[u_, b1], writes=[vn_])
                                b2 = tb()
                                mm(b2, slice(0, 128), qTc, Sb[h][:], [qkvT, Sb[h]])
                                mm(b2, slice(128, 256), xa[:, 128:256], vn_[:], [xa, vn_])
                                mm(b2, slice(256, 384), kd_[:], vn_[:], [kd_, vn_])
                                ot = otmp[i2][h]
                                op("act", lambda e: e.activation(out=ot[:], in_=b2[:, 0:128], func=AF.Copy, scale=e_[:, h:h + 1]),
                                   reads=[b2, e_], writes=[ot])
                                oa = o_all[i2]
                                op("dve", lambda e: e.tensor_tensor(out=oa[:, h, :], in0=ot[:], in1=b2[:, 128:256], op=ALU.add),
                                   reads=[ot, b2], writes=[oa])
                                op("dve", lambda e: e.scalar_tensor_tensor(out=Sf[h][:], in0=Sf[h][:], scalar=e_[:, 8 + h:9 + h],
                                                                           in1=b2[:, 256:384], op0=ALU.mult, op1=ALU.add),
                                   reads=[Sf[h], e_, b2], writes=[Sf[h]])
                                op("pool", lambda e: e.tensor_copy(out=Sb[h][:], in_=Sf[h][:]), reads=[Sf[h]], writes=[Sb[h]])
                            if 0 < STOP <= 7: continue
                            oa, osq, r_ = o_all[i2], o_sq[i2], r4[i2]
                            op("pool", lambda e: e.tensor_tensor(out=osq[:], in0=oa[:], in1=oa[:], op=ALU.mult), reads=[oa], writes=[osq])
                            op("dve", lambda e: e.tensor_reduce(out=r_[:], in_=osq[:], axis=AX.X, op=ALU.add), reads=[osq], writes=[r_])
                            op("act", lambda e: e.activation(out=r_[:], in_=r_[:], func=AF.Sqrt, bias=EPS, scale=1.0 / 128),
                               reads=[r_], writes=[r_])
                            op("dve", lambda e: e.reciprocal(out=r_[:], in_=r_[:]), reads=[r_], writes=[r_])
                            op("dve", lambda e: e.tensor_tensor(out=osq[:], in0=oa[:], in1=r_[:].unsqueeze(2).to_broadcast([128, 4, 128]),
                                                                op=ALU.mult), reads=[oa, r_, osq], writes=[osq])
                            ya_ = ya[i2]
                            op("pool", lambda e: e.tensor_tensor(out=ya_[:], in0=osq[:].rearrange("p h d -> p (h d)"), in1=zA[:, n, :],
                                                                 op=ALU.mult), reads=[osq, zA], writes=[ya_])
                            bT = tb()
                            for h in range(4):
                                tr(bT, ts(h), ya_[:, ts(h)], [ya_])
                            yT_ = yaT[i2]
                            op("act", lambda e: e.copy(out=yT_[:], in_=bfv(bT)[:, 0:512].rearrange("p (h t) -> p h t", h=4)),
                               reads=[bT], writes=[yT_])
                            dma("sp", yT_d[0].rearrange("(h p) t -> p h t", p=128)[:, :, ts(n)], yT_[:], reads=[yT_], writes=[yT_b[0]])
                        P.barrier()

            if "B" in phases:
                with ExitStack() as ph:
                    u_tok = P.sb("u_tok", [128, NT + 1, 512], BF16, ph)
                    zsT = P.sb("zsT", [128, 4, T], BF16, ph)
                    wu = P.sb("wu", [128, 8, 512], BF16, ph)
                    wzb = [P.sb("wzb%d" % i, [128, 8, 128], BF16, ph) for i in range(2)]
                    pw = P.sb("pw", [128, 4, 128], BF16, ph)
                    zst = [P.sb("zst%d" % i, [128, 512], F32, ph) for i in range(2)]
                    pooledT = [P.sb("pooledT%d" % i, [128, 4, 128], BF16, ph) for i in range(2)]
                    ybT = [P.sb("ybT%d" % i, [128, 4, 128], BF16, ph) for i in range(2)]
                    load_w(wu, O_UB, 512)
                    dma("pool", pw[:], pool_w[l].rearrange("g c d -> c g d"), writes=[pw])
                    op("pool", lambda e: e.memset(u_tok[:, 0, :], 0.0), writes=[u_tok])
                    for n in range(NT):
                        bk = tb()
                        for k in range(8):
                            mm(bk, slice(0, 512), hT[:, k, ts(n)], wu[:, k, :], [wu, hT], k == 0, k == 7)
                        op("act", lambda e: e.copy(out=u_tok[:, n + 1, :], in_=bk[:]), reads=[bk], writes=[u_tok])
                    for g in range(4):
                        w_ = wzb[g % 2]
                        load_w(w_, O_ZB + g * 128, 128)
                        for tg in range(4):
                            bk = tb()
                            for k in range(8):
                                mm(bk, slice(0, 512), w_[:, k, :], hT[:, k, ts(tg, 512)], [w_, hT], k == 0, k == 7)
                            z_ = zst[tg % 2]
                            op("act", lambda e: e.activation(out=z_[:], in_=bk[:], func=AF.Silu), reads=[bk], writes=[z_])
                            op("pool", lambda e: e.tensor_scalar(out=zsT[:, g, ts(tg, 512)], in0=z_[:], scalar1=pp[:, PP_PS + g:PP_PS + g + 1],
                                                                 scalar2=None, op0=ALU.mult), reads=[z_, pp], writes=[zsT])
                    for n in range(NT):
                        bk = tb()
                        for g in range(4):
                            mt = lambda kind: cB[:, CB_MT + (g * 3 + kind) * 128:CB_MT + (g * 3 + kind + 1) * 128]
                            if n == 0:
                                mm(bk, ts(g), u_tok[:, 1, ts(g)], mt(2), [u_tok, cB])
                            else:
                                mm(bk, ts(g), u_tok[:, n + 1, ts(g)], mt(0), [u_tok, cB], True, False)
                                mm(bk, ts(g), u_tok[:, n, ts(g)], mt(1), [u_tok, cB], False, True)
                        pl = pooledT[n % 2]
                        op("act", lambda e: e.copy(out=pl[:], in_=bk[:].rearrange("p (g t) -> p g t", g=4)), reads=[bk], writes=[pl])
                        bk2 = tb()
                        for g in range(4):
                            mm(bk2, ts(g), pw[:, g, :], pl[:, g, :], [pw, pl])
                        yb_ = ybT[n % 2]
                        op("dve", lambda e: e.tensor_tensor(out=yb_[:], in0=bk2[:].rearrange("p (g t) -> p g t", g=4), in1=zsT[:, :, ts(n)],
                                                            op=ALU.mult), reads=[bk2, zsT], writes=[yb_])
                        dma("sp", yT_d[1].rearrange("(h p) t -> p h t", p=128)[:, :, ts(n)], yb_[:], reads=[yb_], writes=[yT_b[1]])
                P.barrier()

            if "C" in phases:
                with ExitStack() as ph:
                    kT = P.sb("kT", [128, 4, T], BF16, ph)
                    qT = P.sb("qT", [128, 4, T], BF16, ph)
                    v_tok = P.sb("v_tok", [128, NT, 512], BF16, ph)
                    zcT = P.sb("zcT", [128, 4, T], BF16, ph)
                    qiT = P.sb("qiT", [64, 4, T], BF16, ph)
                    kiT = P.sb("kiT", [64, T], BF16, ph)
                    wiT = P.sb("wiT", [128, NT, 4], F32, ph)
                    with ExitStack() as ph1:
                        ropeA = P.sb("ropeA", [128, 16, 256], F32, ph1)
                        ropeI = P.sb("ropeI", [128, 16, 128], F32, ph1)
                        dma("sp", ropeA[:], ropeA_d, writes=[ropeA])
                        dma("sp", ropeI[:], ropeI_d, writes=[ropeI])
                        wc = [P.sb("wc%d" % i, [128, 8, 512], BF16, ph1) for i in range(2)]
                        sqt = P.sb("sqt", [128, 4, 128], F32, ph1)
                        r4c = P.sb("r4c", [128, 4], F32, ph1)
                        qn = P.sb("qn", [128, 4, 128], F32, ph1)
                        ra = P.sb("ra", [128, 4, 128], F32, ph1)
                        rb = P.sb("rb", [128, 4, 128], F32, ph1)
                        qr = [P.sb("qr%d" % i, [128, 512], BF16, ph1) for i in range(2)]
                        ia = P.sb("ia", [128, 5, 64], F32, ph1)
                        ib = P.sb("ib", [128, 5, 64], F32, ph1)
                        ir = [P.sb("ir%d" % i, [128, 320], BF16, ph1) for i in range(2)]
                        for wi_, (which, c0) in enumerate((("q", O_QC), ("k", O_KC))):
                            w_ = wc[wi_ % 2]
                            load_w(w_, c0, 512)
                            dstT = qT if which == "q" else kT
                            nB = (pp[:, PP_QN:PP_QN + 128] if which == "q" else pp[:, PP_KN:PP_KN + 128]).unsqueeze(1).to_broadcast([128, 4, 128])
                            for n in range(NT):
                                bk = tb()
                                for k in range(8):
                                    mm(bk, slice(0, 512), hT[:, k, ts(n)], w_[:, k, :], [w_, hT], k == 0, k == 7)
                                bk3 = bk[:].rearrange("p (h d) -> p h d", h=4)
                                op("act", lambda e: e.activation(out=sqt[:], in_=bk3, func=AF.Square), reads=[bk], writes=[sqt])
                                op("dve", lambda e: e.tensor_reduce(out=r4c[:], in_=sqt[:], axis=AX.X, op=ALU.add), reads=[sqt], writes=[r4c])
                                op("act", lambda e: e.activation(out=r4c[:], in_=r4c[:], func=AF.Sqrt, bias=EPS, scale=1.0 / 128),
                                   reads=[r4c], writes=[r4c])
                                op("dve", lambda e: e.reciprocal(out=r4c[:], in_=r4c[:]), reads=[r4c], writes=[r4c])
                                op("dve", lambda e: e.tensor_tensor(out=qn[:], in0=bk3, in1=r4c[:].unsqueeze(2).to_broadcast([128, 4, 128]),
                                                                    op=ALU.mult), reads=[bk, r4c], writes=[qn])
                                op("pool", lambda e: e.tensor_tensor(out=qn[:], in0=qn[:], in1=nB, op=ALU.mult), reads=[qn, pp], writes=[qn])
                                cosb = ropeA[:, n, 0:128].unsqueeze(1).to_broadcast([128, 4, 128])
                                s1 = ropeA[:, n, 128:192].unsqueeze(1).to_broadcast([128, 4, 64])
                                s2 = ropeA[:, n, 192:256].unsqueeze(1).to_broadcast([128, 4, 64])
                                op("pool", lambda e: e.tensor_tensor(out=ra[:], in0=qn[:], in1=cosb, op=ALU.mult), reads=[qn, ropeA], writes=[ra])
                                op("dve", lambda e: e.tensor_tensor(out=rb[:, :, 0:64], in0=qn[:, :, 64:128], in1=s1, op=ALU.mult),
                                   reads=[qn, ropeA], writes=[rb])
                                op("dve", lambda e: e.tensor_tensor(out=rb[:, :, 64:128], in0=qn[:, :, 0:64], in1=s2, op=ALU.mult),
                                   reads=[qn, ropeA, rb], writes=[rb])
                                qr_ = qr[n % 2]
                                op("pool", lambda e: e.tensor_tensor(out=qr_[:].rearrange("p (h d) -> p h d", h=4), in0=ra[:], in1=rb[:], op=ALU.add),
                                   reads=[ra, rb], writes=[qr_])
                                bT = tb()
                                for h in range(4):
                                    tr(bT, ts(h), qr_[:, ts(h)], [qr_])
                                op("act", lambda e: e.copy(out=dstT[:, :, ts(n)], in_=bfv(bT)[:, 0:512].rearrange("p (h t) -> p h t", h=4)),
                                   reads=[bT], writes=[dstT])
                        w_ = wc[0]
                        load_w(w_, O_VC, 512)
                        for n in range(NT):
                            bk = tb()
                            for k in range(8):
                                mm(bk, slice(0, 512), hT[:, k, ts(n)], w_[:, k, :], [w_, hT], k == 0, k == 7)
                            op("act", lambda e: e.copy(out=v_tok[:, n, :], in_=bk[:]), reads=[bk], writes=[v_tok])
                        w_ = wc[1]
                        load_w(w_, O_IQ, 324)
                        for n in range(NT):
                            bk = tb()
                            for k in range(8):
                                mm(bk, slice(0, 324), hT[:, k, ts(n)], w_[:, k, 0:324], [w_, hT], k == 0, k == 7)
                            op("dve", lambda e: e.tensor_scalar(out=wiT[:, n, :], in0=bk[:, 320:324], scalar1=1.0 / 16, scalar2=None, op0=ALU.mult),
                               reads=[bk], writes=[wiT])
                            xi = bk[:, 0:320].rearrange("p (h d) -> p h d", h=5)
                            cosb = ropeI[:, n, 0:64].unsqueeze(1).to_broadcast([128, 5, 64])
                            s1 = ropeI[:, n, 64:96].unsqueeze(1).to_broadcast([128, 5, 32])
                            s2 = ropeI[:, n, 96:128].unsqueeze(1).to_broadcast([128, 5, 32])
                            op("dve", lambda e: e.tensor_tensor(out=ia[:], in0=xi, in1=cosb, op=ALU.mult), reads=[bk, ropeI], writes=[ia])
                            op("dve", lambda e: e.tensor_tensor(out=ib[:, :, 0:32], in0=xi[:, :, 32:64], in1=s1, op=ALU.mult),
                               reads=[bk, ropeI], writes=[ib])
                            op("dve", lambda e: e.tensor_tensor(out=ib[:, :, 32:64], in0=xi[:, :, 0:32], in1=s2, op=ALU.mult),
                               reads=[bk, ropeI, ib], writes=[ib])
                            ir_ = ir[n % 2]
                            op("pool", lambda e: e.tensor_tensor(out=ir_[:].rearrange("p (h d) -> p h d", h=5), in0=ia[:], in1=ib[:], op=ALU.add),
                               reads=[ia, ib], writes=[ir_])
                            bT = tb()
                            for j in range(5):
                                op("pe", lambda e: e.transpose(bfv(bT)[0:64, ts(j)], ir_[:, j * 64:(j + 1) * 64], idB), reads=[ir_, cB], writes=[bT])
                            op("act", lambda e: e.copy(out=qiT[:, :, ts(n)], in_=bfv(bT)[0:64, 0:512].rearrange("p (h t) -> p h t", h=4)),
                               reads=[bT], writes=[qiT])
                            op("act", lambda e: e.copy(out=kiT[:, ts(n)], in_=bfv(bT)[0:64, 512:640]), reads=[bT], writes=[kiT])
                        for g in range(4):
                            w_ = wc[g % 2]
                            load_w(w_, O_ZC + g * 128, 128)
                            for tg in range(4):
                                bk = tb()
                                for k in range(8):
                                    mm(bk, slice(0, 512), w_[:, k, 0:128], hT[:, k, ts(tg, 512)], [w_, hT], k == 0, k == 7)
                                op("act", lambda e: e.activation(out=zcT[:, g, ts(tg, 512)], in_=bk[:], func=AF.Silu), reads=[bk], writes=[zcT])
                        P.barrier()
                    with ExitStack() as ph2:
                        score = [P.sb("score%d" % i, [128, T], F32, ph2) for i in range(2)]
                        relu_t = [P.sb("relu%d" % i, [128, 512], F32, ph2) for i in range(2)]
                        cmpj = P.sb("cmpj", [128, T], BF16, ph2)
                        maskts = [P.sb("maskts%d" % i, [128, T], BF16, ph2) for i in range(2)]
                        maskT = [P.sb("maskT%d" % i, [128, NT, 128], BF16, ph2) for i in range(2)]
                        eT = [P.sb("eT%d" % i, [128, 512], BF16, ph2) for i in range(3)]
                        pT = [P.sb("pT%d" % i, [128, 512], BF16, ph2) for i in range(3)]
                        bis = [P.sb("bis%d" % i, [128, 8], F32, ph2) for i in range(2)]
                        rden = P.sb("rden", [128, 512], F32, ph2)
                        otc = P.sb("otc", [128, 512], F32, ph2)
                        ycT_t = [P.sb("ycT%d" % i, [128, 4, 128], BF16, ph2) for i in range(2)]
                        state["pool"] = list(range(6))
                        Ob, Dn = banks[6], banks[7]
                        cnt_e = 0
                        for qb in range(NT):
                            nk = 128 * (qb + 1)
                            sc = score[qb % 2]
                            for c0 in range(0, nk, 512):
                                c1 = min(nk, c0 + 512)
                                w = c1 - c0
                                for h in range(4):
                                    bk = tb()
                                    mm(bk, slice(0, w), qiT[0:64, h, ts(qb)], kiT[0:64, c0:c1], [qiT, kiT])
                                    rl = relu_t[h % 2]
                                    op("act", lambda e: e.activation(out=rl[:, 0:w], in_=bk[:, 0:w], func=AF.Relu), reads=[bk], writes=[rl])
                                    if h == 0:
                                        op("dve", lambda e: e.tensor_scalar(out=sc[:, c0:c1], in0=rl[:, 0:w], scalar1=wiT[:, qb, 0:1], scalar2=None,
                                                                            op0=ALU.mult), reads=[rl, wiT], writes=[sc])
                                    else:
                                        op("dve", lambda e: e.scalar_tensor_tensor(out=sc[:, c0:c1], in0=rl[:, 0:w], scalar=wiT[:, qb, h:h + 1],
                                                                                   in1=sc[:, c0:c1], op0=ALU.mult, op1=ALU.add),
                                           reads=[rl, wiT, sc], writes=[sc])
                            op("pool", lambda e: e.tensor_tensor(out=sc[:, qb * 128:nk], in0=sc[:, qb * 128:nk], in1=negM, op=ALU.add),
                               reads=[sc, cF], writes=[sc])
                            if qb >= 2:
                                b_ = bis[qb % 2]
                                op("dve", lambda e: e.tensor_reduce(out=b_[:, 1:2], in_=sc[:, 0:nk], axis=AX.X, op=ALU.max), reads=[sc], writes=[b_])
                                op("dve", lambda e: e.tensor_reduce(out=b_[:, 0:1], in_=sc[:, 0:nk - 128], axis=AX.X, op=ALU.min), reads=[sc, b_], writes=[b_])
                                op("dve", lambda e: e.tensor_tensor(out=b_[:, 1:2], in0=b_[:, 1:2], in1=b_[:, 0:1], op=ALU.subtract), reads=[b_], writes=[b_])
                                for it in range(NBIS):
                                    op("dve", lambda e: e.tensor_scalar(out=b_[:, 1:2], in0=b_[:, 1:2], scalar1=0.5, scalar2=None, op0=ALU.mult),
                                       reads=[b_], writes=[b_])
                                    op("dve", lambda e: e.tensor_tensor(out=b_[:, 2:3], in0=b_[:, 0:1], in1=b_[:, 1:2], op=ALU.add), reads=[b_], writes=[b_])
                                    op("dve", lambda e: e.memset(b_[:, 3:4], 0.0), reads=[b_], writes=[b_])
                                    op("dve", lambda e: e.tensor_scalar(out=cmpj[:, 0:nk], in0=sc[:, 0:nk], scalar1=b_[:, 2:3], scalar2=0.0,
                                                                        op0=ALU.is_ge, op1=ALU.add, accum_out=b_[:, 3:4]),
                                       reads=[sc, b_], writes=[cmpj, b_])
                                    op("dve", lambda e: e.tensor_scalar(out=b_[:, 4:5], in0=b_[:, 3:4], scalar1=float(TOPK), scalar2=b_[:, 1:2],
                                                                        op0=ALU.is_ge, op1=ALU.mult), reads=[b_], writes=[b_])
                                    op("dve", lambda e: e.tensor_tensor(out=b_[:, 0:1], in0=b_[:, 0:1], in1=b_[:, 4:5], op=ALU.add), reads=[b_], writes=[b_])
                                thr, thr_r = b_[:, 0:1], [b_]
                            else:
                                thr, thr_r = -1e29, []
                            mts = maskts[qb % 2]
                            op("dve", lambda e: e.tensor_scalar(out=mts[:, 0:nk], in0=sc[:, 0:nk], scalar1=thr, scalar2=None, op0=ALU.is_ge),
                               reads=[sc] + thr_r, writes=[mts])
                            mT = maskT[qb % 2]
                            for kb0 in range(0, qb + 1, 4):
                                nb = min(4, qb + 1 - kb0)
                                bT = tb()
                                for j in range(nb):
                                    tr(bT, ts(j), mts[:, ts(kb0 + j)], [mts])
                                op("act", lambda e: e.copy(out=mT[:, kb0:kb0 + nb, :], in_=bfv(bT)[:, 0:nb * 128].rearrange("p (g t) -> p g t", g=nb)),
                                   reads=[bT], writes=[mT])
                            for kb in range(qb + 1):
                                bk = tb()
                                for h in range(4):
                                    mm(bk, ts(h), kT[:, h, ts(kb)], qT[:, h, ts(qb)], [kT, qT])
                                e_ = eT[cnt_e % 3]
                                p_ = pT[cnt_e % 3]
                                cnt_e += 1
                                op("act", lambda e: e.activation(out=e_[:], in_=bk[:], func=AF.Exp, scale=128.0 ** -0.5), reads=[bk], writes=[e_])
                                op("pool", lambda e: e.tensor_tensor(out=p_[:].rearrange("p (h t) -> p h t", h=4),
                                                                     in0=e_[:].rearrange("p (h t) -> p h t", h=4),
                                                                     in1=mT[:, kb, :].unsqueeze(1).to_broadcast([128, 4, 128]), op=ALU.mult),
                                   reads=[e_, mT], writes=[p_])
                                for h in range(4):
                                    mm(Ob, ts(h), v_tok[:, kb, ts(h)], p_[:, ts(h)], [v_tok, p_], kb == 0 and h == 0, kb == qb)
                                mm(Dn, slice(0, 512), onesB, p_[:], [cB, p_], kb == 0, kb == qb)
                            op("dve", lambda e: e.reciprocal(out=rden[:], in_=Dn[:]), reads=[Dn], writes=[rden])
                            op("dve", lambda e: e.tensor_tensor(out=otc[:], in0=Ob[:], in1=rden[:], op=ALU.mult), reads=[Ob, rden], writes=[otc])
                            yc_ = ycT_t[qb % 2]
                            op("pool", lambda e: e.tensor_tensor(out=yc_[:], in0=otc[:].rearrange("p (h t) -> p h t", h=4), in1=zcT[:, :, ts(qb)],
                                                                 op=ALU.mult), reads=[otc, zcT], writes=[yc_])
                            dma("sp", yT_d[2].rearrange("(h p) t -> p h t", p=128)[:, :, ts(qb)], yc_[:], reads=[yc_], writes=[yT_b[2]])
                        state["pool"] = list(range(8))
                P.barrier()

            if "M" in phases:
                with ExitStack() as ph:
                    mergedT = P.sb("mergedT", [128, 8, T], BF16, ph)
                    woutb = P.sb("woutb", [128, 8, D], BF16, ph)
                    dma("pool", woutb[:], w_out[l].rearrange("(k p) e -> p k e", p=128), writes=[woutb])
                    wg = [[P.sb("wg%d_%d" % (s_, i), [128, 8, 128], BF16, ph) for i in range(3)] for s_ in range(2)]
                    wbr = [[P.sb("wbr%d_%d" % (s_, i), [128, 4, 128], BF16, ph) for i in range(3)] for s_ in range(2)]
                    ysl = [[P.sb("ysl%d_%d" % (s_, i), [128, 4, 512], BF16, ph) for i in range(3)] for s_ in range(2)]
                    sg = [P.sb("sg%d" % i, [128, 512], F32, ph) for i in range(3)]
                    macc = [P.sb("macc%d" % i, [128, 512], F32, ph) for i in range(2)]
                    xt2 = [P.sb("xt2_%d" % i, [128, D], F32, ph) for i in range(2)]
                    ot2 = [P.sb("ot2_%d" % i, [128, D], F32, ph) for i in range(2)]
                    it_ = 0
                    for dc in range(8):
                        s_ = dc % 2
                        for i in range(3):
                            dma("pool", wg[s_][i][:], wv[:, :, O_G + i * 1024 + dc * 128:O_G + i * 1024 + (dc + 1) * 128], writes=[wg[s_][i]])
                            dma("pool", wbr[s_][i][:], w_br[l, i].rearrange("(k p) d -> p k d", p=128)[:, :, ts(dc)], writes=[wbr[s_][i]])
                        for tg in range(4):
                            y_ = ysl[it_ % 2]
                            it_ += 1
                            for i in range(3):
                                dma("sp", y_[i][:], yT_d[i].rearrange("(h p) t -> p h t", p=128)[:, :, ts(tg, 512)], reads=[yT_b[i]], writes=[y_[i]])
                            for i in range(3):
                                bg = tb()
                                for k in range(8):
                                    mm(bg, slice(0, 512), wg[s_][i][:, k, :], hT[:, k, ts(tg, 512)], [wg[s_][i], hT], k == 0, k == 7)
                                op("act", lambda e: e.activation(out=sg[i][:], in_=bg[:], func=AF.Sigmoid,
                                                                 bias=pp[:, PP_GB + i * 8 + dc:PP_GB + i * 8 + dc + 1], scale=1.0),
                                   reads=[bg, pp], writes=[sg[i]])
                                bp = tb()
                                for k in range(4):
                                    mm(bp, slice(0, 512), wbr[s_][i][:, k, :], y_[i][:, k, :], [wbr[s_][i], y_[i]], k == 0, k == 3)
                                op("dve", lambda e: e.tensor_tensor(out=sg[i][:], in0=sg[i][:], in1=bp[:], op=ALU.mult), reads=[sg[i], bp], writes=[sg[i]])
                            ma = macc[tg % 2]
                            op("pool", lambda e: e.tensor_tensor(out=ma[:], in0=sg[0][:], in1=sg[1][:], op=ALU.add), reads=[sg[0], sg[1]], writes=[ma])
                            op("pool", lambda e: e.tensor_tensor(out=mergedT[:, dc, ts(tg, 512)], in0=ma[:], in1=sg[2][:], op=ALU.add),
                               reads=[ma, sg[2]], writes=[mergedT])
                    for n in range(NT):
                        xt_ = xt2[n % 2]
                        ot_ = ot2[n % 2]
                        dma("sp", xt_[:], xsrc[ts(n), :], reads=[xsrc_b] if xsrc_b else [], writes=[xt_])
                        for half in range(2):
                            bk = tb()
                            for k in range(8):
                                mm(bk, slice(0, 512), mergedT[:, k, ts(n)], woutb[:, k, ts(half, 512)], [mergedT, woutb], k == 0, k == 7)
                            op("dve", lambda e: e.tensor_tensor(out=ot_[:, ts(half, 512)], in0=bk[:], in1=xt_[:, ts(half, 512)], op=ALU.add),
                               reads=[bk, xt_], writes=[ot_])
                        dma("sp", xdst[ts(n), :], ot_[:], reads=[ot_], writes=[xdst_b])
                P.barrier()

        P.wait_all("sp", [out_b, xs_b] + yT_b)
        for e in ("act", "dve", "pool", "pe"):
            pass
        print("ops", P.n_ops, "waits", P.n_waits, "dsems", P.n_dsem)
    return nc


def _consts():
    i = np.arange(128)
    ident = np.eye(128, dtype=np.float32)
    triu = (i[:, None] <= i[None, :]).astype(np.float32)
    ones = np.ones((128, 128), np.float32)
    maskA = np.where(i[None, :] > i[:, None], 1e4, 0.0).astype(np.float32)
    neg = np.where(i[None, :] > i[:, None], -1e30, 0.0).astype(np.float32)
    cf = np.concatenate([ident, triu, ones, maskA, neg], axis=1)
    strict = (i[None, :] < i[:, None]).astype(np.float32)
    mts = []
    for w in (2, 4, 8, 16):
        t = i[:, None]
        s = i[None, :]
        cur = np.where((s <= t) & (s > t - w), 1.0 / w, 0.0) - np.eye(128)
        prev = np.where(s > t - w + 128, 1.0 / w, 0.0)
        cnt = np.minimum(t + 1, w).astype(np.float64)
        first = np.where((s <= t) & (s > t - w), 1.0 / cnt, 0.0) - np.eye(128)
        mts += [cur.T, prev.T, first.T]
    cb = np.concatenate([ident, ones, strict] + [m.astype(np.float32) for m in mts], axis=1).astype(ml_dtypes.bfloat16)

    def rope(dim):
        inv = (np.float32(10000.0) ** (-(np.arange(0, dim, 2, dtype=np.float32) / np.float32(dim)))).astype(np.float32)
        ang = np.arange(T, dtype=np.float32)[:, None] * inv[None, :]
        c, s = np.cos(ang).astype(np.float32), np.sin(ang).astype(np.float32)
        tab = np.concatenate([c, c, -s, s], axis=1)
        return np.ascontiguousarray(tab.reshape(NT, 128, 2 * dim).transpose(1, 0, 2))
    return np.ascontiguousarray(cf), np.ascontiguousarray(cb), rope(128), rope(64)


def _pp(norm_g, gate_b, conv_w, a_log, dt_bias, dn_onorm, pool_scale, q_norm, k_norm):
    L = norm_g.shape[0]
    pp = np.zeros((L, 128, PP_W), np.float32)
    for l in range(L):
        pp[l, :, PP_CONV:PP_CONV + 48] = conv_w[l].reshape(4, 12, 128).transpose(2, 1, 0).reshape(128, 48)
        pp[l, :, PP_GB:PP_GB + 24] = gate_b[l].reshape(3, 8, 128).transpose(2, 0, 1).reshape(128, 24)
        pp[l, :, PP_PS:PP_PS + 4] = pool_scale[l].reshape(4, 128).T
        pp[l, :, PP_ALOG:PP_ALOG + 4] = a_log[l][None, :]
        pp[l, :, PP_DTB:PP_DTB + 4] = dt_bias[l][None, :]
        pp[l, :, PP_ON:PP_ON + 128] = dn_onorm[l][None, :]
        pp[l, :, PP_QN:PP_QN + 128] = q_norm[l][None, :]
        pp[l, :, PP_KN:PP_KN + 128] = k_norm[l][None, :]
        pp[l, :, PP_G:PP_G + D] = norm_g[l][None, :]
    return pp


def make_in_maps(inputs, cores):
    f = lambda k: np.ascontiguousarray(np.asarray(inputs[k], dtype=np.float32))
    cf, cb, ropeA, ropeI = _consts()
    pp = _pp(f("norm_g"), f("gate_b"), f("conv_w"), f("a_log"), f("dt_bias"), f("dn_onorm"), f("pool_scale"),
             f("q_norm"), f("k_norm"))
    x = f("x")
    shared = {"w_in": f("w_in"), "w_branch": f("w_branch"), "w_out": f("w_out"), "pool_w": f("pool_w"),
              "pp": pp, "cf": cf, "cb": cb, "ropeA": ropeA, "ropeI": ropeI}
    return [dict(shared, x=np.ascontiguousarray(x[c])) for c in cores]


def kernel(**inputs):
    nc = build(2)
    cores = list(range(8))
    res = run_bass_kernel_spmd(nc, make_in_maps(inputs, cores), core_ids=cores)
    return np.stack([np.asarray(r["out"], dtype=np.float32) for r in res.results], axis=0)
```

```python
from contextlib import ExitStack
import os
import numpy as np
import ml_dtypes
import concourse.bass as bass
import concourse.mybir as mybir
from concourse.bass_utils import run_bass_kernel_spmd

F32 = mybir.dt.float32
BF16 = mybir.dt.bfloat16
ALU = mybir.AluOpType
AF = mybir.ActivationFunctionType
AX = mybir.AxisListType

STOP = int(os.environ.get('STOP', 0))
T = 2048
D = 1024
NT = 16
INW = 8524
EPS = 1e-6
TOPK = 256
NBIS = 18
O_QA, O_KA, O_VA, O_ZA, O_BA = 0, 512, 1024, 1536, 2048
O_UB, O_ZB = 2056, 2568
O_QC, O_KC, O_VC, O_ZC, O_IQ = 3080, 3592, 4104, 4616, 5128
O_G = 5452
PP_CONV, PP_GB, PP_PS, PP_ALOG, PP_DTB, PP_ON, PP_QN, PP_KN, PP_G = 0, 48, 72, 76, 80, 84, 212, 340, 468
PP_W = 468 + 1024
CF_ID, CF_TRIU, CF_ONES, CF_MASKA, CF_NEG = 0, 128, 256, 384, 512
CB_ID, CB_ONES, CB_STRICT, CB_MT = 0, 128, 256, 384


class Buf:
    __slots__ = ("name", "t", "lw", "rd", "dsem", "excl")

    def __init__(self, name, t=None, excl=False):
        self.name = name
        self.t = t
        self.excl = excl
        self.lw = None
        self.rd = {}
        self.dsem = None

    def __getitem__(self, k):
        return self.t[k]


class Prog:
    COMPUTE = ("pe", "act", "dve", "pool")

    def __init__(self, nc, es, same_sync=("act", "dve", "pool")):
        self.nc = nc
        self.es = es
        self.eng = {"pe": nc.tensor, "act": nc.scalar, "dve": nc.vector, "pool": nc.gpsimd, "sp": nc.sync}
        self.sem = {}
        self.cnt = {}
        self.seen = {}
        self.same_sync = set(same_sync)
        for e in self.COMPUTE:
            self.sem[e] = es.enter_context(nc.semaphore("s_" + e))
            self.cnt[e] = 0
        self.n_dsem = 0
        self.n_ops = 0
        self.n_waits = 0

    def sb(self, name, shape, dt, stack=None):
        self.n_names = getattr(self, "n_names", 0) + 1
        name = "sb%d_%s" % (self.n_names, name)
        t = (stack or self.es).enter_context(self.nc.sbuf_tensor(name, shape, dt))
        return Buf(name, t)

    def ps(self, name, shape, dt, stack=None):
        t = (stack or self.es).enter_context(self.nc.psum_tensor(name, shape, dt))
        return Buf(name, t, excl=True)

    def dsem_for(self, buf):
        if buf.dsem is None:
            key = "d%d" % self.n_dsem
            self.n_dsem += 1
            self.sem[key] = self.es.enter_context(self.nc.semaphore("s_" + key))
            self.cnt[key] = 0
            buf.dsem = key
        return buf.dsem

    def _waits(self, issuer, selfkey, reads, writes):
        need = {}
        for b in reads:
            if b.lw is not None:
                e, n = b.lw
                if need.get(e, 0) < n:
                    need[e] = n
            if b.excl:
                for e, n in b.rd.items():
                    if e != selfkey and need.get(e, 0) < n:
                        need[e] = n
        for b in writes:
            if b.lw is not None:
                e, n = b.lw
                if need.get(e, 0) < n:
                    need[e] = n
            for e, n in b.rd.items():
                if need.get(e, 0) < n:
                    need[e] = n
        seen = self.seen.setdefault(issuer, {})
        for e, n in need.items():
            if e == selfkey and e not in self.same_sync:
                continue
            if seen.get(e, 0) < n:
                self.eng[issuer].wait_ge(self.sem[e], n)
                seen[e] = n
                self.n_waits += 1

    def _commit(self, key, n, reads, writes):
        for b in reads:
            if b.rd.get(key, 0) < n:
                b.rd[key] = n
        for b in writes:
            b.lw = (key, n)
            b.rd = {}

    def op(self, eng, emit, reads=(), writes=()):
        self._waits(eng, eng, reads, writes)
        ins = emit(self.eng[eng])
        self.cnt[eng] += 1
        ins.then_inc(self.sem[eng], 1)
        self._commit(eng, self.cnt[eng], reads, writes)
        self.n_ops += 1
        return ins

    def dma(self, queue, out, in_, reads=(), writes=(), sembuf=None, **kw):
        sbf = sembuf if sembuf is not None else writes[0]
        key = self.dsem_for(sbf)
        self._waits(queue, None, reads, writes)
        ins = self.eng[queue].dma_start(out=out, in_=in_, **kw)
        self.cnt[key] += 16
        ins.then_inc(self.sem[key], 16)
        self._commit(key, self.cnt[key], reads, writes)
        self.n_ops += 1
        return ins

    def wait_all(self, issuer, bufs):
        self._waits(issuer, issuer, bufs, ())

    def barrier(self):
        for e in self.COMPUTE:
            seen = self.seen.setdefault(e, {})
            for e2 in self.COMPUTE:
                if e2 == e:
                    continue
                n = self.cnt[e2]
                if n > 0 and seen.get(e2, 0) < n:
                    self.eng[e].wait_ge(self.sem[e2], n)
                    seen[e2] = n
            for k2, n in self.cnt.items():
                if k2 in self.COMPUTE or n == 0:
                    continue
                if seen.get(k2, 0) < n:
                    self.eng[e].wait_ge(self.sem[k2], n)
                    seen[k2] = n


def run_il(gens):
    gens = list(gens)
    while gens:
        for g in list(gens):
            try:
                next(g)
            except StopIteration:
                gens.remove(g)


def run_window(gen_iter, width):
    it = iter(gen_iter)
    active = []
    done = False
    while True:
        while not done and len(active) < width:
            try:
                active.append(next(it))
            except StopIteration:
                done = True
        if not active:
            break
        for g in list(active):
            try:
                next(g)
            except StopIteration:
                active.remove(g)


def ts(i, n=128):
    return slice(i * n, (i + 1) * n)


def build(n_layers=2, phases="NA123BCM", debug=False):
    nc = bass.Bass("TRN2", target_bir_lowering=False)
    dt_in = lambda name, shape, dt=F32: nc.dram_tensor(name, shape, dt, kind="ExternalInput").ap()
    x_in = dt_in("x", [T, D])
    w_in = dt_in("w_in", [2, D, INW])
    w_br = dt_in("w_branch", [2, 3, 512, D])
    w_out = dt_in("w_out", [2, D, D])
    pool_w = dt_in("pool_w", [2, 4, 128, 128])
    pp_d = dt_in("pp", [2, 128, PP_W])
    cf_d = dt_in("cf", [128, 640])
    cb_d = dt_in("cb", [128, 1920], BF16)
    ropeA_d = dt_in("ropeA", [128, 16, 256])
    ropeI_d = dt_in("ropeI", [128, 16, 128])
    out_d = nc.dram_tensor("out", [T, D], F32, kind="ExternalOutput").ap()
    skind = "ExternalOutput" if debug else "Internal"
    xs_d = nc.dram_tensor("xs", [T, D], F32, kind="Internal").ap()
    yT_d = nc.dram_tensor("yT", [3, 512, T], BF16, kind=skind).ap()

    with ExitStack() as es:
        P = Prog(nc, es)
        op, dma = P.op, P.dma
        cF = P.sb("cF", [128, 640], F32)
        cB = P.sb("cB", [128, 1920], BF16)
        hT = P.sb("hT", [128, 8, T], BF16)
        pp = P.sb("pp", [128, PP_W], F32)
        banks = [P.ps("bk%d" % i, [128, 512], F32) for i in range(8)]
        xs_b = Buf("xs_d")
        yT_b = [Buf("yT%d" % i) for i in range(3)]
        out_b = Buf("out_d")
        state = {"tb": 0, "pool": list(range(8))}

        def tb():
            lst = state["pool"]
            b = banks[lst[state["tb"] % len(lst)]]
            state["tb"] += 1
            return b

        def bfv(bank):
            return bank.t[:].bitcast(BF16)

        dma("sp", cF[:], cf_d, writes=[cF])
        dma("sp", cB[:], cb_d, writes=[cB])
        idF = cF[:, CF_ID:CF_ID + 128]
        triuF = cF[:, CF_TRIU:CF_TRIU + 128]
        onesF = cF[:, CF_ONES:CF_ONES + 128]
        maskA = cF[:, CF_MASKA:CF_MASKA + 128]
        negM = cF[:, CF_NEG:CF_NEG + 128]
        idB = cB[:, CB_ID:CB_ID + 128]
        onesB = cB[:, CB_ONES:CB_ONES + 128]
        strictB = cB[:, CB_STRICT:CB_STRICT + 128]

        def mm(bank, osl, lhsT, rhs, reads, start=True, stop=True):
            op("pe", lambda e: e.matmul(bank.t[:, osl], lhsT=lhsT, rhs=rhs, start=start, stop=stop),
               reads=reads, writes=[bank])

        def tr(bank, osl, in_, reads):
            op("pe", lambda e: e.transpose(bfv(bank)[:, osl], in_, idB), reads=list(reads) + [cB], writes=[bank])

        for l in range(n_layers):
            xsrc, xsrc_b = (x_in, None) if l == 0 else (xs_d, xs_b)
            xdst, xdst_b = (xs_d, xs_b) if l < n_layers - 1 else (out_d, out_b)
            dma("sp", pp[:], pp_d[l], writes=[pp])
            wv = w_in[l].rearrange("(k p) e -> p k e", p=128)

            with ExitStack() as ph:
                xt = [P.sb("xt%d" % i, [128, D], F32, ph) for i in range(2)]
                junk = P.sb("njunk", [128, D], BF16, ph)
                hb = [P.sb("hb%d" % i, [128, D], BF16, ph) for i in range(2)]
                ssq2 = [P.sb("ssq%d" % i, [128, 2 * NT], F32, ph) for i in range(2)]
                for q_ in ssq2:
                    op("dve", lambda e: e.memset(q_[:], 0.0), writes=[q_])
                def g_norm(n):
                    ssq = ssq2[n % 2]
                    xb_ = xt[n % 2]
                    dma("sp", xb_[:], xsrc[ts(n), :], reads=[xsrc_b] if xsrc_b else [], writes=[xb_])
                    yield
                    op("act", lambda e: e.activation(out=junk[:], in_=xb_[:], func=AF.Square,
                                                     accum_out=ssq[:, n:n + 1]), reads=[xb_], writes=[junk, ssq])
                    yield
                    op("act", lambda e: e.activation(out=ssq[:, NT + n:NT + n + 1], in_=ssq[:, n:n + 1], func=AF.Sqrt,
                                                     bias=EPS, scale=1.0 / D), reads=[ssq], writes=[ssq])
                    yield
                    op("dve", lambda e: e.reciprocal(out=ssq[:, NT + n:NT + n + 1], in_=ssq[:, NT + n:NT + n + 1]),
                       reads=[ssq], writes=[ssq])
                    yield
                    hb_ = hb[n % 2]
                    op("dve", lambda e: e.scalar_tensor_tensor(out=hb_[:], in0=xb_[:], scalar=ssq[:, NT + n:NT + n + 1],
                                                               in1=pp[:, PP_G:PP_G + D], op0=ALU.mult, op1=ALU.mult),
                       reads=[xb_, ssq, pp], writes=[hb_])
                    yield
                    bk = tb()
                    for k in range(8):
                        tr(bk, ts(k), hb_[:, ts(k)], [hb_])
                        yield
                    op("act", lambda e: e.copy(out=hT[:, :, ts(n)], in_=bfv(bk).rearrange("p (k t) -> p k t", k=8)),
                       reads=[bk], writes=[hT])
                    yield
                run_window((g_norm(n) for n in range(NT)), 2)
            P.barrier()

            def load_w(dst, c0, ncols, src=None):
                dma("pool", dst[:, :, 0:ncols], wv[:, :, c0:c0 + ncols], writes=[dst])

            if "A" in phases:
                with ExitStack() as ph:
                    qkvT = P.sb("qkvT", [128, 12, T], BF16, ph)
                    zA = P.sb("zA", [128, NT, 512], BF16, ph)
                    raw8 = P.sb("raw8", [128, NT, 8], F32, ph)
                    betaT = P.sb("betaT", [128, NT, 4], F32, ph)
                    gT = P.sb("gT", [128, NT, 4], F32, ph)
                    with ExitStack() as ph1:
                        wsl = [P.sb("wA%d" % i, [128, 8, 128], BF16, ph1) for i in range(2)]
                        cbuf = [P.sb("cbuf%d" % i, [128, 3 + T], F32, ph1) for i in range(2)]
                        acc2 = [P.sb("acc%d" % i, [128, T], F32, ph1) for i in range(2)]
                        sl2 = [P.sb("sl%d" % i, [128, T], F32, ph1) for i in range(2)]
                        sq2 = [P.sb("sq%d" % i, [128, T], BF16, ph1) for i in range(2)]
                        rn2 = [[P.sb("rn%d_%d" % (j, i), [128, 512], F32, ph1) for i in range(2)] for j in range(2)]
                        for c_ in cbuf:
                            op("pool", lambda e: e.memset(c_[:, 0:3], 0.0), writes=[c_])
                        def g_conv(cc):
                            acc, sl, sq, rn = acc2[cc % 2], sl2[cc % 2], sq2[cc % 2], rn2[cc % 2]
                            wb = wsl[cc % 2]
                            load_w(wb, cc * 128, 128)
                            yield
                            cb_ = cbuf[cc % 2]
                            for tg in range(4):
                                bk = tb()
                                for k in range(8):
                                    mm(bk, slice(0, 512), wb[:, k, :], hT[:, k, ts(tg, 512)], [wb, hT], k == 0, k == 7)
                                    yield
                                op("act", lambda e: e.copy(out=cb_[:, 3 + tg * 512:3 + (tg + 1) * 512], in_=bk[:]),
                                   reads=[bk], writes=[cb_])
                                yield
                            cw = lambda j: pp[:, PP_CONV + cc * 4 + j:PP_CONV + cc * 4 + j + 1]
                            op("pool", lambda e: e.tensor_scalar(out=acc[:], in0=cb_[:, 0:T], scalar1=cw(0), scalar2=None,
                                                                 op0=ALU.mult), reads=[cb_, pp], writes=[acc])
                            yield
                            op("dve", lambda e: e.scalar_tensor_tensor(out=acc[:], in0=cb_[:, 1:1 + T], scalar=cw(1), in1=acc[:],
                                                                        op0=ALU.mult, op1=ALU.add), reads=[cb_, pp, acc], writes=[acc])
                            yield
                            op("dve", lambda e: e.scalar_tensor_tensor(out=acc[:], in0=cb_[:, 2:2 + T], scalar=cw(2), in1=acc[:],
                                                                       op0=ALU.mult, op1=ALU.add), reads=[cb_, pp, acc], writes=[acc])
                            yield
                            op("dve", lambda e: e.scalar_tensor_tensor(out=acc[:], in0=cb_[:, 3:3 + T], scalar=cw(3), in1=acc[:],
                                                                       op0=ALU.mult, op1=ALU.add), reads=[cb_, pp, acc], writes=[acc])
                            yield
                            if cc >= 8:
                                op("act", lambda e: e.activation(out=qkvT[:, cc, :], in_=acc[:], func=AF.Silu),
                                   reads=[acc], writes=[qkvT])
                                yield
                            else:
                                op("act", lambda e: e.activation(out=sl[:], in_=acc[:], func=AF.Silu), reads=[acc], writes=[sl])
                                yield
                                op("pool", lambda e: e.tensor_tensor(out=sq[:], in0=sl[:], in1=sl[:], op=ALU.mult),
                                   reads=[sl], writes=[sq])
                                yield
                                for tg in range(4):
                                    bk = tb()
                                    mm(bk, slice(0, 512), onesB, sq[:, ts(tg, 512)], [cB, sq])
                                    yield
                                    rn_ = rn[tg % 2]
                                    op("act", lambda e: e.activation(out=rn_[:], in_=bk[:], func=AF.Sqrt, bias=EPS, scale=1.0),
                                       reads=[bk], writes=[rn_])
                                    yield
                                    op("dve", lambda e: e.reciprocal(out=rn_[:], in_=rn_[:]), reads=[rn_], writes=[rn_])
                                    yield
                                    sc = 128.0 ** -0.5 if cc < 4 else 1.0
                                    op("dve", lambda e: e.scalar_tensor_tensor(out=qkvT[:, cc, ts(tg, 512)], in0=sl[:, ts(tg, 512)],
                                                                               scalar=sc, in1=rn_[:], op0=ALU.mult, op1=ALU.mult),
                                       reads=[sl, rn_], writes=[qkvT])
                                    yield
                        run_window((g_conv(cc) for cc in range(12 if '1' in phases else 0)), 2)
                        wz = P.sb("wz", [128, 8, 520], BF16, ph1)
                        load_w(wz, O_ZA, 520)
                        zt = [P.sb("zt%d" % i, [128, 512], BF16, ph1) for i in range(2)]
                        onB = pp[:, PP_ON:PP_ON + 128].unsqueeze(1).to_broadcast([128, 4, 128])
                        def g_za(n):
                            bk = tb()
                            bk2 = tb()
                            for k in range(8):
                                mm(bk, slice(0, 512), hT[:, k, ts(n)], wz[:, k, 0:512], [wz, hT], k == 0, k == 7)
                                yield
                            for k in range(8):
                                mm(bk2, slice(0, 8), hT[:, k, ts(n)], wz[:, k, 512:520], [wz, hT], k == 0, k == 7)
                                yield
                            zt_ = zt[n % 2]
                            op("act", lambda e: e.activation(out=zt_[:], in_=bk[:], func=AF.Silu), reads=[bk], writes=[zt_])
                            yield
                            op("pool", lambda e: e.tensor_tensor(out=zA[:, n, :].rearrange("p (h d) -> p h d", h=4),
                                                                 in0=zt_[:].rearrange("p (h d) -> p h d", h=4), in1=onB, op=ALU.mult),
                               reads=[zt_, pp], writes=[zA])
                            yield
                            op("dve", lambda e: e.tensor_copy(out=raw8[:, n, :], in_=bk2[:, 0:8]), reads=[bk2], writes=[raw8])
                            yield
                        run_window((g_za(n) for n in range(NT if '2' in phases else 0)), 2)
                        op("act", lambda e: e.activation(out=betaT[:], in_=raw8[:, :, 0:4], func=AF.Sigmoid), reads=[raw8], writes=[betaT])
                        dtb = pp[:, PP_DTB:PP_DTB + 4].unsqueeze(1).to_broadcast([128, NT, 4])
                        op("dve", lambda e: e.tensor_tensor(out=gT[:], in0=raw8[:, :, 4:8], in1=dtb, op=ALU.add), reads=[raw8, pp], writes=[gT])
                        op("act", lambda e: e.activation(out=gT[:], in_=gT[:], func=AF.Exp), reads=[gT], writes=[gT])
                        op("act", lambda e: e.activation(out=gT[:], in_=gT[:], func=AF.Ln, bias=1.0, scale=1.0), reads=[gT], writes=[gT])
                        negA = P.sb("negA", [128, 4], F32, ph1)
                        op("act", lambda e: e.activation(out=negA[:], in_=pp[:, PP_ALOG:PP_ALOG + 4], func=AF.Exp), reads=[pp], writes=[negA])
                        op("dve", lambda e: e.tensor_scalar(out=negA[:], in0=negA[:], scalar1=-1.0, scalar2=None, op0=ALU.mult),
                           reads=[negA], writes=[negA])
                        op("dve", lambda e: e.tensor_tensor(out=gT[:], in0=gT[:], in1=negA[:].unsqueeze(1).to_broadcast([128, NT, 4]),
                                                            op=ALU.mult), reads=[gT, negA], writes=[gT])
                        P.barrier()
                    with ExitStack() as ph2:
                        Sf = [P.sb("Sf%d" % h, [128, 128], F32, ph2) for h in range(4)]
                        Sb = [P.sb("Sb%d" % h, [128, 128], BF16, ph2) for h in range(4)]
                        for h in range(4):
                            op("pool", lambda e: e.memset(Sf[h][:], 0.0), writes=[Sf[h]])
                            op("pool", lambda e: e.memset(Sb[h][:], 0.0), writes=[Sb[h]])
                        NB_ = 2
                        mk = lambda nm, shp, dt: [[P.sb("%s_%d_%d" % (nm, i, h), shp, dt, ph2) for h in range(4)] for i in range(NB_)]
                        g12 = [P.sb("g12_%d" % i, [128, 12], F32, ph2) for i in range(NB_)]
                        e12 = [P.sb("e12_%d" % i, [128, 12], F32, ph2) for i in range(NB_)]
                        be4 = [P.sb("be4_%d" % i, [128, 4], F32, ph2) for i in range(NB_)]
                        rhsg = mk("rhsg", [128, 128], F32)
                        dec = mk("dec", [128, 128], F32)
                        decs = mk("decs", [128, 128], F32)
                        Lb = mk("Lb", [128, 128], BF16)
                        At = mk("At", [128, 128], BF16)
                        XA = mk("XA", [128, 256], BF16)
                        kbg = mk("kbg", [128, 128], BF16)
                        kd = mk("kd", [128, 128], BF16)
                        vb = mk("vb", [128, 128], BF16)
                        XY = [mk("XY%d" % j, [128, 256], BF16) for j in range(2)]
                        Pm = [mk("Pm%d" % j, [128, 128], BF16) for j in range(2)]
                        wTs = mk("wTs", [128, 128], BF16)
                        us = mk("us", [128, 128], F32)
                        vn = mk("vn", [128, 128], BF16)
                        otmp = mk("otmp", [128, 128], F32)
                        o_all = [P.sb("o_all%d" % i, [128, 4, 128], F32, ph2) for i in range(NB_)]
                        o_sq = [P.sb("o_sq%d" % i, [128, 4, 128], F32, ph2) for i in range(NB_)]
                        r4 = [P.sb("r4_%d" % i, [128, 4], F32, ph2) for i in range(NB_)]
                        ya = [P.sb("ya%d" % i, [128, 512], BF16, ph2) for i in range(NB_)]
                        yaT = [P.sb("yaT%d" % i, [128, 4, 128], BF16, ph2) for i in range(NB_)]
                        for n in range(int(os.environ.get('A3N', NT)) if '3' in phases else 0):
                            i2 = n % NB_
                            bk = tb()
                            mm(bk, slice(0, 4), triuF, gT[:, n, :], [cF, gT])
                            mm(bk, slice(4, 8), onesF, gT[:, n, :], [cF, gT])
                            g_, e_ = g12[i2], e12[i2]
                            op("dve", lambda e: e.tensor_copy(out=g_[:, 0:4], in_=bk[:, 0:4]), reads=[bk], writes=[g_])
                            op("dve", lambda e: e.tensor_copy(out=g_[:, 8:12], in_=bk[:, 4:8]), reads=[bk], writes=[g_])
                            op("dve", lambda e: e.tensor_tensor(out=g_[:, 4:8], in0=g_[:, 8:12], in1=g_[:, 0:4], op=ALU.subtract),
                               reads=[g_], writes=[g_])
                            op("act", lambda e: e.activation(out=e_[:], in_=g_[:], func=AF.Exp), reads=[g_], writes=[e_])
                            b4 = be4[i2]
                            op("dve", lambda e: e.tensor_tensor(out=b4[:], in0=betaT[:, n, :], in1=e_[:, 0:4], op=ALU.mult),
                               reads=[betaT, e_], writes=[b4])
                            def head_gen(h, n=n, i2=i2, g_=g_, e_=e_, b4=b4):
                                kTc = qkvT[:, 4 + h, ts(n)]
                                qTc = qkvT[:, h, ts(n)]
                                vTc = qkvT[:, 8 + h, ts(n)]
                                rg, dc, dcs = rhsg[i2][h], dec[i2][h], decs[i2][h]
                                op("pool", lambda e: e.tensor_scalar(out=rg[:], in0=triuF, scalar1=gT[:, n, h:h + 1], scalar2=None,
                                                                     op0=ALU.mult), reads=[cF, gT], writes=[rg])
                                yield
                                bD = tb()
                                mm(bD, slice(0, 128), onesF, rg[:], [cF, rg], True, False)
                                yield
                                mm(bD, slice(0, 128), idF, maskA, [cF], False, True)
                                yield
                                op("act", lambda e: e.activation(out=dc[:], in_=bD[:, 0:128], func=AF.Exp, scale=-1.0,
                                                                 bias=g_[:, h:h + 1]), reads=[bD, g_], writes=[dc])
                                yield
                                op("pool", lambda e: e.tensor_tensor(out=dcs[:], in0=dc[:], in1=strictB, op=ALU.mult),
                                   reads=[dc, cB], writes=[dcs])
                                yield
                                bK = tb()
                                mm(bK, slice(0, 128), kTc, kTc, [qkvT])
                                yield
                                mm(bK, slice(128, 256), qTc, kTc, [qkvT])
                                yield
                                L_, A_ = Lb[i2][h], At[i2][h]
                                op("dve", lambda e: e.scalar_tensor_tensor(out=L_[:], in0=bK[:, 0:128], scalar=betaT[:, n, h:h + 1],
                                                                           in1=dcs[:], op0=ALU.mult, op1=ALU.mult),
                                   reads=[bK, betaT, dcs], writes=[L_])
                                yield
                                op("dve", lambda e: e.tensor_tensor(out=A_[:], in0=bK[:, 128:256], in1=dc[:], op=ALU.mult),
                                   reads=[bK, dc], writes=[A_])
                                yield
                                bT = tb()
                                tr(bT, slice(0, 128), L_[:], [L_])
                                yield
                                tr(bT, slice(128, 256), A_[:], [A_])
                                yield
                                tr(bT, slice(256, 384), kTc, [qkvT])
                                yield
                                tr(bT, slice(384, 512), vTc, [qkvT])
                                yield
                                xa = XA[i2][h]
                                op("act", lambda e: e.copy(out=xa[:], in_=bfv(bT)[:, 0:256]), reads=[bT], writes=[xa])
                                yield
                                kbg_, kd_, vb_ = kbg[i2][h], kd[i2][h], vb[i2][h]
                                op("act", lambda e: e.activation(out=kbg_[:], in_=bfv(bT)[:, 256:384], func=AF.Copy,
                                                                 scale=b4[:, h:h + 1]), reads=[bT, b4], writes=[kbg_])
                                yield
                                op("dve", lambda e: e.tensor_scalar(out=kd_[:], in0=bfv(bT)[:, 256:384], scalar1=e_[:, 4 + h:5 + h],
                                                                    scalar2=None, op0=ALU.mult), reads=[bT, e_], writes=[kd_])
                                yield
                                op("dve", lambda e: e.tensor_scalar(out=vb_[:], in0=bfv(bT)[:, 384:512], scalar1=betaT[:, n, h:h + 1],
                                                                    scalar2=None, op0=ALU.mult), reads=[bT, betaT], writes=[vb_])
                                yield
                                p_ = Pm[0][i2][h]
                                op("dve", lambda e: e.tensor_tensor(out=p_[:], in0=idB, in1=xa[:, 0:128], op=ALU.subtract),
                                   reads=[cB, xa], writes=[p_])
                                yield
                                Xc, Yc, Xb, Yb = xa[:, 0:128], L_[:], xa, L_
                                for m in range(int(os.environ.get('NEU', 6))):
                                    xy = XY[m % 2][i2][h]
                                    bN = tb()
                                    mm(bN, slice(0, 128), Yc, Xc, [Xb, Yb])
                                    yield
                                    mm(bN, slice(128, 256), Xc, Yc, [Xb, Yb])
                                    yield
                                    op("act" if m % 2 == 0 else "dve",
                                       (lambda e: e.copy(out=xy[:], in_=bN[:, 0:256])) if m % 2 == 0 else
                                       (lambda e: e.tensor_copy(out=xy[:], in_=bN[:, 0:256])), reads=[bN], writes=[xy])
                                    yield
                                    Xc, Yc, Xb, Yb = xy[:, 0:128], xy[:, 128:256], xy, xy
                                    pn = Pm[(m + 1) % 2][i2][h]
                                    bP = tb()
                                    mm(bP, slice(0, 128), idB, p_[:], [cB, p_], True, False)
                                    yield
                                    mm(bP, slice(0, 128), Yc, p_[:], [xy, p_], False, True)
                                    yield
                                    op("dve" if m % 2 == 0 else "act",
                                       (lambda e: e.tensor_copy(out=pn[:], in_=bP[:, 0:128])) if m % 2 == 0 else
                                       (lambda e: e.copy(out=pn[:], in_=bP[:, 0:128])), reads=[bP], writes=[pn])
                                    yield
                                    p_ = pn
                                TT = p_
                                bW = tb()
                                mm(bW, slice(0, 128), kbg_[:], TT[:], [kbg_, TT])
                                yield
                                mm(bW, slice(128, 256), TT[:], vb_[:], [TT, vb_])
                                yield
                                w_, u_ = wTs[i2][h], us[i2][h]
                                op("act", lambda e: e.copy(out=w_[:], in_=bW[:, 0:128]), reads=[bW], writes=[w_])
                                yield
                                op("dve", lambda e: e.tensor_copy(out=u_[:], in_=bW[:, 128:256]), reads=[bW], writes=[u_])
                                yield
                                b1 = tb()
                                mm(b1, slice(0, 128), w_[:], Sb[h][:], [w_, Sb[h]])
                                yield
                                vn_ = vn[i2][h]
                                op("dve", lambda e: e.tensor_tensor(out=vn_[:], in0=u_[:], in1=b1[:, 0:128], op=ALU.subtract),
                                   reads=[u_, b1], writes=[vn_])
                                yield
                                b2 = tb()
                                mm(b2, slice(0, 128), qTc, Sb[h][:], [qkvT, Sb[h]])
                                yield
                                mm(b2, slice(128, 256), xa[:, 128:256], vn_[:], [xa, vn_])
                                yield
                                mm(b2, slice(256, 384), kd_[:], vn_[:], [kd_, vn_])
                                yield
                                ot = otmp[i2][h]
                                op("act", lambda e: e.activation(out=ot[:], in_=b2[:, 0:128], func=AF.Copy, scale=e_[:, h:h + 1]),
                                   reads=[b2, e_], writes=[ot])
                                yield
                                oa = o_all[i2]
                                op("dve", lambda e: e.tensor_tensor(out=oa[:, h, :], in0=ot[:], in1=b2[:, 128:256], op=ALU.add),
                                   reads=[ot, b2], writes=[oa])
                                yield
                                op("dve", lambda e: e.scalar_tensor_tensor(out=Sf[h][:], in0=Sf[h][:], scalar=e_[:, 8 + h:9 + h],
                                                                           in1=b2[:, 256:384], op0=ALU.mult, op1=ALU.add),
                                   reads=[Sf[h], e_, b2], writes=[Sf[h]])
                                yield
                                op("pool", lambda e: e.tensor_copy(out=Sb[h][:], in_=Sf[h][:]), reads=[Sf[h]], writes=[Sb[h]])
                                yield
                            run_il([head_gen(h) for h in range(4)])
                            oa, osq, r_ = o_all[i2], o_sq[i2], r4[i2]
                            op("pool", lambda e: e.tensor_tensor(out=osq[:], in0=oa[:], in1=oa[:], op=ALU.mult), reads=[oa], writes=[osq])
                            op("dve", lambda e: e.tensor_reduce(out=r_[:], in_=osq[:], axis=AX.X, op=ALU.add), reads=[osq], writes=[r_])
                            op("act", lambda e: e.activation(out=r_[:], in_=r_[:], func=AF.Sqrt, bias=EPS, scale=1.0 / 128),
                               reads=[r_], writes=[r_])
                            op("dve", lambda e: e.reciprocal(out=r_[:], in_=r_[:]), reads=[r_], writes=[r_])
                            op("dve", lambda e: e.tensor_tensor(out=osq[:], in0=oa[:], in1=r_[:].unsqueeze(2).to_broadcast([128, 4, 128]),
                                                                op=ALU.mult), reads=[oa, r_, osq], writes=[osq])
                            ya_ = ya[i2]
                            op("pool", lambda e: e.tensor_tensor(out=ya_[:], in0=osq[:].rearrange("p h d -> p (h d)"), in1=zA[:, n, :],
                                                                 op=ALU.mult), reads=[osq, zA], writes=[ya_])
                            bT = tb()
                            for h in range(4):
                                tr(bT, ts(h), ya_[:, ts(h)], [ya_])
                            yT_ = yaT[i2]
                            op("act", lambda e: e.copy(out=yT_[:], in_=bfv(bT)[:, 0:512].rearrange("p (h t) -> p h t", h=4)),
                               reads=[bT], writes=[yT_])
                            dma("sp", yT_d[0].rearrange("(h p) t -> p h t", p=128)[:, :, ts(n)], yT_[:], reads=[yT_], writes=[yT_b[0]])
                        P.barrier()

            if "B" in phases:
                with ExitStack() as ph:
                    u_tok = P.sb("u_tok", [128, NT + 1, 512], BF16, ph)
                    zsT = P.sb("zsT", [128, 4, T], BF16, ph)
                    wu = P.sb("wu", [128, 8, 512], BF16, ph)
                    wzb = [P.sb("wzb%d" % i, [128, 8, 128], BF16, ph) for i in range(2)]
                    pw = P.sb("pw", [128, 4, 128], BF16, ph)
                    zst = [P.sb("zst%d" % i, [128, 512], F32, ph) for i in range(2)]
                    pooledT = [P.sb("pooledT%d" % i, [128, 4, 128], BF16, ph) for i in range(2)]
                    ybT = [P.sb("ybT%d" % i, [128, 4, 128], BF16, ph) for i in range(2)]
                    load_w(wu, O_UB, 512)
                    dma("pool", pw[:], pool_w[l].rearrange("g c d -> c g d"), writes=[pw])
                    op("pool", lambda e: e.memset(u_tok[:, 0, :], 0.0), writes=[u_tok])
                    for n in range(NT):
                        bk = tb()
                        for k in range(8):
                            mm(bk, slice(0, 512), hT[:, k, ts(n)], wu[:, k, :], [wu, hT], k == 0, k == 7)
                        op("act", lambda e: e.copy(out=u_tok[:, n + 1, :], in_=bk[:]), reads=[bk], writes=[u_tok])
                    for g in range(4):
                        w_ = wzb[g % 2]
                        load_w(w_, O_ZB + g * 128, 128)
                        for tg in range(4):
                            bk = tb()
                            for k in range(8):
                                mm(bk, slice(0, 512), w_[:, k, :], hT[:, k, ts(tg, 512)], [w_, hT], k == 0, k == 7)
                            z_ = zst[tg % 2]
                            op("act", lambda e: e.activation(out=z_[:], in_=bk[:], func=AF.Silu), reads=[bk], writes=[z_])
                            op("pool", lambda e: e.tensor_scalar(out=zsT[:, g, ts(tg, 512)], in0=z_[:], scalar1=pp[:, PP_PS + g:PP_PS + g + 1],
                                                                 scalar2=None, op0=ALU.mult), reads=[z_, pp], writes=[zsT])
                    def g_pool(n):
                        bk = tb()
                        for g in range(4):
                            mt = lambda kind: cB[:, CB_MT + (g * 3 + kind) * 128:CB_MT + (g * 3 + kind + 1) * 128]
                            if n == 0:
                                mm(bk, ts(g), u_tok[:, 1, ts(g)], mt(2), [u_tok, cB])
                                yield
                            else:
                                mm(bk, ts(g), u_tok[:, n + 1, ts(g)], mt(0), [u_tok, cB], True, False)
                                yield
                                mm(bk, ts(g), u_tok[:, n, ts(g)], mt(1), [u_tok, cB], False, True)
                                yield
                        pl = pooledT[n % 2]
                        op("act", lambda e: e.copy(out=pl[:], in_=bk[:].rearrange("p (g t) -> p g t", g=4)), reads=[bk], writes=[pl])
                        yield
                        bk2 = tb()
                        for g in range(4):
                            mm(bk2, ts(g), pw[:, g, :], pl[:, g, :], [pw, pl])
                            yield
                        yb_ = ybT[n % 2]
                        op("dve", lambda e: e.tensor_tensor(out=yb_[:], in0=bk2[:].rearrange("p (g t) -> p g t", g=4), in1=zsT[:, :, ts(n)],
                                                            op=ALU.mult), reads=[bk2, zsT], writes=[yb_])
                        yield
                        dma("sp", yT_d[1].rearrange("(h p) t -> p h t", p=128)[:, :, ts(n)], yb_[:], reads=[yb_], writes=[yT_b[1]])
                        yield
                    run_window((g_pool(n) for n in range(NT)), 2)
                P.barrier()

            if "C" in phases:
                with ExitStack() as ph:
                    kT = P.sb("kT", [128, 4, T], BF16, ph)
                    qT = P.sb("qT", [128, 4, T], BF16, ph)
                    v_tok = P.sb("v_tok", [128, NT, 512], BF16, ph)
                    zcT = P.sb("zcT", [128, 4, T], BF16, ph)
                    qiT = P.sb("qiT", [64, 4, T], BF16, ph)
                    kiT = P.sb("kiT", [64, T], BF16, ph)
                    wiT = P.sb("wiT", [128, NT, 4], F32, ph)
                    with ExitStack() as ph1:
                        ropeA = P.sb("ropeA", [128, 16, 256], F32, ph1)
                        ropeI = P.sb("ropeI", [128, 16, 128], F32, ph1)
                        dma("sp", ropeA[:], ropeA_d, writes=[ropeA])
                        dma("sp", ropeI[:], ropeI_d, writes=[ropeI])
                        wc = [P.sb("wc%d" % i, [128, 8, 512], BF16, ph1) for i in range(2)]
                        ia2 = [P.sb("ia%d" % i, [128, 5, 64], F32, ph1) for i in range(2)]
                        ib2 = [P.sb("ib%d" % i, [128, 5, 64], F32, ph1) for i in range(2)]
                        ir = [P.sb("ir%d" % i, [128, 320], BF16, ph1) for i in range(2)]
                        for wi_, c0 in enumerate((O_QC, O_KC)):
                            load_w(wc[wi_], c0, 512)
                        sqt2 = [P.sb("sqt%d" % i, [128, 4, 128], F32, ph1) for i in range(2)]
                        r4c2 = [P.sb("r4c%d" % i, [128, 4], F32, ph1) for i in range(2)]
                        qn2 = [P.sb("qn%d" % i, [128, 4, 128], F32, ph1) for i in range(2)]
                        rb2 = [P.sb("rb%d" % i, [128, 4, 128], F32, ph1) for i in range(2)]
                        qr2 = [[P.sb("qr%d_%d" % (j, i), [128, 512], BF16, ph1) for i in range(2)] for j in range(2)]
                        def g_qk(item):
                            wi_, n = item
                            w_ = wc[wi_]
                            dstT = qT if wi_ == 0 else kT
                            nB = (pp[:, PP_QN:PP_QN + 128] if wi_ == 0 else pp[:, PP_KN:PP_KN + 128]).unsqueeze(1).to_broadcast([128, 4, 128])
                            sqt, r4c, qn, rb, qr = sqt2[wi_], r4c2[wi_], qn2[wi_], rb2[wi_], qr2[wi_]
                            ra = sqt
                            bk = tb()
                            for k in range(8):
                                mm(bk, slice(0, 512), hT[:, k, ts(n)], w_[:, k, :], [w_, hT], k == 0, k == 7)
                                yield
                            bk3 = bk[:].rearrange("p (h d) -> p h d", h=4)
                            op("act", lambda e: e.activation(out=sqt[:], in_=bk3, func=AF.Square), reads=[bk], writes=[sqt])
                            yield
                            op("dve", lambda e: e.tensor_reduce(out=r4c[:], in_=sqt[:], axis=AX.X, op=ALU.add), reads=[sqt], writes=[r4c])
                            yield
                            op("act", lambda e: e.activation(out=r4c[:], in_=r4c[:], func=AF.Sqrt, bias=EPS, scale=1.0 / 128),
                               reads=[r4c], writes=[r4c])
                            yield
                            op("dve", lambda e: e.reciprocal(out=r4c[:], in_=r4c[:]), reads=[r4c], writes=[r4c])
                            yield
                            op("dve", lambda e: e.tensor_tensor(out=qn[:], in0=bk3, in1=r4c[:].unsqueeze(2).to_broadcast([128, 4, 128]),
                                                                op=ALU.mult), reads=[bk, r4c], writes=[qn])
                            yield
                            op("pool", lambda e: e.tensor_tensor(out=qn[:], in0=qn[:], in1=nB, op=ALU.mult), reads=[qn, pp], writes=[qn])
                            yield
                            cosb = ropeA[:, n, 0:128].unsqueeze(1).to_broadcast([128, 4, 128])
                            s1 = ropeA[:, n, 128:192].unsqueeze(1).to_broadcast([128, 4, 64])
                            s2 = ropeA[:, n, 192:256].unsqueeze(1).to_broadcast([128, 4, 64])
                            op("pool", lambda e: e.tensor_tensor(out=ra[:], in0=qn[:], in1=cosb, op=ALU.mult), reads=[qn, ropeA], writes=[ra])
                            yield
                            op("dve", lambda e: e.tensor_tensor(out=rb[:, :, 0:64], in0=qn[:, :, 64:128], in1=s1, op=ALU.mult),
                               reads=[qn, ropeA], writes=[rb])
                            yield
                            op("dve", lambda e: e.tensor_tensor(out=rb[:, :, 64:128], in0=qn[:, :, 0:64], in1=s2, op=ALU.mult),
                               reads=[qn, ropeA, rb], writes=[rb])
                            yield
                            qr_ = qr[n % 2]
                            op("pool", lambda e: e.tensor_tensor(out=qr_[:].rearrange("p (h d) -> p h d", h=4), in0=ra[:], in1=rb[:], op=ALU.add),
                               reads=[ra, rb], writes=[qr_])
                            yield
                            bT = tb()
                            for h in range(4):
                                tr(bT, ts(h), qr_[:, ts(h)], [qr_])
                                yield
                            op("act", lambda e: e.copy(out=dstT[:, :, ts(n)], in_=bfv(bT)[:, 0:512].rearrange("p (h t) -> p h t", h=4)),
                               reads=[bT], writes=[dstT])
                            yield
                        run_window((g_qk((wi_, n)) for n in range(NT) for wi_ in range(2)), 2)
                        w_ = wc[0]
                        load_w(w_, O_VC, 512)
                        def g_v(n):
                            bk = tb()
                            for k in range(8):
                                mm(bk, slice(0, 512), hT[:, k, ts(n)], w_[:, k, :], [w_, hT], k == 0, k == 7)
                                yield
                            op("act", lambda e: e.copy(out=v_tok[:, n, :], in_=bk[:]), reads=[bk], writes=[v_tok])
                            yield
                        run_window((g_v(n) for n in range(NT)), 2)
                        w_ = wc[1]
                        load_w(w_, O_IQ, 324)
                        def g_idx(n):
                            ia, ib = ia2[n % 2], ib2[n % 2]
                            bk = tb()
                            for k in range(8):
                                mm(bk, slice(0, 324), hT[:, k, ts(n)], w_[:, k, 0:324], [w_, hT], k == 0, k == 7)
                                yield
                            op("dve", lambda e: e.tensor_scalar(out=wiT[:, n, :], in0=bk[:, 320:324], scalar1=1.0 / 16, scalar2=None, op0=ALU.mult),
                               reads=[bk], writes=[wiT])
                            yield
                            xi = bk[:, 0:320].rearrange("p (h d) -> p h d", h=5)
                            cosb = ropeI[:, n, 0:64].unsqueeze(1).to_broadcast([128, 5, 64])
                            s1 = ropeI[:, n, 64:96].unsqueeze(1).to_broadcast([128, 5, 32])
                            s2 = ropeI[:, n, 96:128].unsqueeze(1).to_broadcast([128, 5, 32])
                            op("dve", lambda e: e.tensor_tensor(out=ia[:], in0=xi, in1=cosb, op=ALU.mult), reads=[bk, ropeI], writes=[ia])
                            yield
                            op("dve", lambda e: e.tensor_tensor(out=ib[:, :, 0:32], in0=xi[:, :, 32:64], in1=s1, op=ALU.mult),
                               reads=[bk, ropeI], writes=[ib])
                            yield
                            op("dve", lambda e: e.tensor_tensor(out=ib[:, :, 32:64], in0=xi[:, :, 0:32], in1=s2, op=ALU.mult),
                               reads=[bk, ropeI, ib], writes=[ib])
                            yield
                            ir_ = ir[n % 2]
                            op("pool", lambda e: e.tensor_tensor(out=ir_[:].rearrange("p (h d) -> p h d", h=5), in0=ia[:], in1=ib[:], op=ALU.add),
                               reads=[ia, ib], writes=[ir_])
                            yield
                            bT = tb()
                            for j in range(5):
                                op("pe", lambda e: e.transpose(bfv(bT)[0:64, ts(j)], ir_[:, j * 64:(j + 1) * 64], idB), reads=[ir_, cB], writes=[bT])
                                yield
                            op("act", lambda e: e.copy(out=qiT[:, :, ts(n)], in_=bfv(bT)[0:64, 0:512].rearrange("p (h t) -> p h t", h=4)),
                               reads=[bT], writes=[qiT])
                            yield
                            op("act", lambda e: e.copy(out=kiT[:, ts(n)], in_=bfv(bT)[0:64, 512:640]), reads=[bT], writes=[kiT])
                            yield
                        run_window((g_idx(n) for n in range(NT)), 2)
                        for g in range(4):
                            w_ = wc[g % 2]
                            load_w(w_, O_ZC + g * 128, 128)
                            for tg in range(4):
                                bk = tb()
                                for k in range(8):
                                    mm(bk, slice(0, 512), w_[:, k, 0:128], hT[:, k, ts(tg, 512)], [w_, hT], k == 0, k == 7)
                                op("act", lambda e: e.activation(out=zcT[:, g, ts(tg, 512)], in_=bk[:], func=AF.Silu), reads=[bk], writes=[zcT])
                        P.barrier()
                    with ExitStack() as ph2:
                        score = [P.sb("score%d" % i, [128, T], F32, ph2) for i in range(2)]
                        relu_t = [P.sb("relu%d" % i, [128, 512], F32, ph2) for i in range(4)]
                        maskts = [P.sb("maskts%d" % i, [128, T], BF16, ph2) for i in range(2)]
                        maskT = [P.sb("maskT%d" % i, [128, NT, 128], BF16, ph2) for i in range(4)]
                        eT = [P.sb("eT%d" % i, [128, 512], BF16, ph2) for i in range(4)]
                        pT = [P.sb("pT%d" % i, [128, 512], BF16, ph2) for i in range(4)]
                        bis = [P.sb("bis%d" % i, [128, 8], F32, ph2) for i in range(2)]
                        rden = [P.sb("rden%d" % i, [128, 512], F32, ph2) for i in range(2)]
                        otc = [P.sb("otc%d" % i, [128, 512], F32, ph2) for i in range(2)]
                        ycT_t = [P.sb("ycT%d" % i, [128, 4, 128], BF16, ph2) for i in range(2)]
                        state["pool"] = list(range(4))
                        cst = {"e": 0, "r": 0}

                        def sel_gen(qb):
                            nk = 128 * (qb + 1)
                            sc = score[qb % 2]
                            for c0 in range(0, nk, 512):
                                c1 = min(nk, c0 + 512)
                                w = c1 - c0
                                for h in range(4):
                                    bk = tb()
                                    mm(bk, slice(0, w), qiT[0:64, h, ts(qb)], kiT[0:64, c0:c1], [qiT, kiT])
                                    yield
                                    rl = relu_t[cst["r"] % 4]
                                    cst["r"] += 1
                                    op("act", lambda e: e.activation(out=rl[:, 0:w], in_=bk[:, 0:w], func=AF.Relu), reads=[bk], writes=[rl])
                                    yield
                                    if h == 0:
                                        op("dve", lambda e: e.tensor_scalar(out=sc[:, c0:c1], in0=rl[:, 0:w], scalar1=wiT[:, qb, 0:1], scalar2=None,
                                                                            op0=ALU.mult), reads=[rl, wiT], writes=[sc])
                                    else:
                                        op("dve", lambda e: e.scalar_tensor_tensor(out=sc[:, c0:c1], in0=rl[:, 0:w], scalar=wiT[:, qb, h:h + 1],
                                                                                   in1=sc[:, c0:c1], op0=ALU.mult, op1=ALU.add),
                                           reads=[rl, wiT, sc], writes=[sc])
                                    yield
                            op("pool", lambda e: e.tensor_tensor(out=sc[:, qb * 128:nk], in0=sc[:, qb * 128:nk], in1=negM, op=ALU.add),
                               reads=[sc, cF], writes=[sc])
                            yield
                            mts = maskts[qb % 2]
                            if qb >= 2:
                                b_ = bis[qb % 2]
                                op("dve", lambda e: e.tensor_reduce(out=b_[:, 1:2], in_=sc[:, 0:nk], axis=AX.X, op=ALU.max), reads=[sc], writes=[b_])
                                yield
                                op("dve", lambda e: e.tensor_reduce(out=b_[:, 0:1], in_=sc[:, 0:nk - 128], axis=AX.X, op=ALU.min), reads=[sc, b_], writes=[b_])
                                yield
                                op("dve", lambda e: e.tensor_tensor(out=b_[:, 1:2], in0=b_[:, 1:2], in1=b_[:, 0:1], op=ALU.subtract), reads=[b_], writes=[b_])
                                yield
                                for it in range(NBIS):
                                    op("dve", lambda e: e.tensor_scalar(out=b_[:, 1:2], in0=b_[:, 1:2], scalar1=0.5, scalar2=None, op0=ALU.mult),
                                       reads=[b_], writes=[b_])
                                    yield
                                    op("dve", lambda e: e.tensor_scalar(out=b_[:, 2:3], in0=b_[:, 0:1], scalar1=b_[:, 1:2], scalar2=-1.0,
                                                                        op0=ALU.add, op1=ALU.mult), reads=[b_], writes=[b_])
                                    yield
                                    op("pool", lambda e: e.memset(b_[:, 3:4], 0.0), reads=[b_], writes=[b_])
                                    yield
                                    op("act", lambda e: e.activation(out=mts[:, 0:nk], in_=sc[:, 0:nk], func=AF.Sign, bias=b_[:, 2:3], scale=1.0,
                                                                     accum_out=b_[:, 3:4]), reads=[sc, b_], writes=[mts, b_])
                                    yield
                                    op("dve", lambda e: e.tensor_scalar(out=b_[:, 4:5], in0=b_[:, 3:4], scalar1=float(2 * TOPK - nk), scalar2=b_[:, 1:2],
                                                                        op0=ALU.is_ge, op1=ALU.mult), reads=[b_], writes=[b_])
                                    yield
                                    op("dve", lambda e: e.tensor_tensor(out=b_[:, 0:1], in0=b_[:, 0:1], in1=b_[:, 4:5], op=ALU.add), reads=[b_], writes=[b_])
                                    yield
                                thr, thr_r = b_[:, 0:1], [b_]
                            else:
                                thr, thr_r = -1e29, []
                            op("dve", lambda e: e.tensor_scalar(out=mts[:, 0:nk], in0=sc[:, 0:nk], scalar1=thr, scalar2=None, op0=ALU.is_ge),
                               reads=[sc] + thr_r, writes=[mts])
                            yield
                            mT = maskT[qb % 4]
                            for kb0 in range(0, qb + 1, 4):
                                nb = min(4, qb + 1 - kb0)
                                bT = tb()
                                for j in range(nb):
                                    tr(bT, ts(j), mts[:, ts(kb0 + j)], [mts])
                                yield
                                op("act", lambda e: e.copy(out=mT[:, kb0:kb0 + nb, :], in_=bfv(bT)[:, 0:nb * 128].rearrange("p (g t) -> p g t", g=nb)),
                                   reads=[bT], writes=[mT])
                                yield

                        def att_gen(qb):
                            mT = maskT[qb % 4]
                            Ob, Dn = banks[4 + 2 * (qb % 2)], banks[5 + 2 * (qb % 2)]
                            for kb in range(qb + 1):
                                bk = tb()
                                for h in range(4):
                                    mm(bk, ts(h), kT[:, h, ts(kb)], qT[:, h, ts(qb)], [kT, qT])
                                yield
                                e_ = eT[cst["e"] % 4]
                                p_ = pT[cst["e"] % 4]
                                cst["e"] += 1
                                op("act", lambda e: e.activation(out=e_[:], in_=bk[:], func=AF.Exp, scale=128.0 ** -0.5), reads=[bk], writes=[e_])
                                yield
                                op("pool", lambda e: e.tensor_tensor(out=p_[:].rearrange("p (h t) -> p h t", h=4),
                                                                     in0=e_[:].rearrange("p (h t) -> p h t", h=4),
                                                                     in1=mT[:, kb, :].unsqueeze(1).to_broadcast([128, 4, 128]), op=ALU.mult),
                                   reads=[e_, mT], writes=[p_])
                                yield
                                for h in range(4):
                                    mm(Ob, ts(h), v_tok[:, kb, ts(h)], p_[:, ts(h)], [v_tok, p_], kb == 0 and h == 0, kb == qb)
                                mm(Dn, slice(0, 512), onesB, p_[:], [cB, p_], kb == 0, kb == qb)
                                yield
                            rd, ot_ = rden[qb % 2], otc[qb % 2]
                            op("dve", lambda e: e.reciprocal(out=rd[:], in_=Dn[:]), reads=[Dn], writes=[rd])
                            yield
                            op("dve", lambda e: e.tensor_tensor(out=ot_[:], in0=Ob[:], in1=rd[:], op=ALU.mult), reads=[Ob, rd], writes=[ot_])
                            yield
                            yc_ = ycT_t[qb % 2]
                            op("pool", lambda e: e.tensor_tensor(out=yc_[:], in0=ot_[:].rearrange("p (h t) -> p h t", h=4), in1=zcT[:, :, ts(qb)],
                                                                 op=ALU.mult), reads=[ot_, zcT], writes=[yc_])
                            yield
                            dma("sp", yT_d[2].rearrange("(h p) t -> p h t", p=128)[:, :, ts(qb)], yc_[:], reads=[yc_], writes=[yT_b[2]])
                            yield

                        for r in range(NT // 2 + 1):
                            gens = []
                            if r >= 1:
                                gens += [att_gen(2 * r - 2), att_gen(2 * r - 1)]
                            if r < NT // 2:
                                gens += [sel_gen(2 * r), sel_gen(2 * r + 1)]
                            run_il(gens)
                        state["pool"] = list(range(8))
                P.barrier()

            if "M" in phases:
                with ExitStack() as ph:
                    mergedT = P.sb("mergedT", [128, 8, T], BF16, ph)
                    woutb = P.sb("woutb", [128, 8, D], BF16, ph)
                    dma("pool", woutb[:], w_out[l].rearrange("(k p) e -> p k e", p=128), writes=[woutb])
                    wg = [[P.sb("wg%d_%d" % (s_, i), [128, 8, 128], BF16, ph) for i in range(3)] for s_ in range(2)]
                    wbr = [[P.sb("wbr%d_%d" % (s_, i), [128, 4, 128], BF16, ph) for i in range(3)] for s_ in range(2)]
                    ysl = [[P.sb("ysl%d_%d" % (s_, i), [128, 4, 512], BF16, ph) for i in range(3)] for s_ in range(2)]
                    macc = [P.sb("macc%d" % i, [128, 512], F32, ph) for i in range(2)]
                    xt2 = [P.sb("xt2_%d" % i, [128, D], F32, ph) for i in range(2)]
                    ot2 = [P.sb("ot2_%d" % i, [128, D], F32, ph) for i in range(2)]
                    sg2 = [[P.sb("sgx%d_%d" % (j, i), [128, 512], F32, ph) for i in range(3)] for j in range(2)]
                    def g_m(item):
                        idx, dc, tg = item
                        s_ = dc % 2
                        sg = sg2[idx % 2]
                        if tg == 0:
                            for i in range(3):
                                dma("pool", wg[s_][i][:], wv[:, :, O_G + i * 1024 + dc * 128:O_G + i * 1024 + (dc + 1) * 128], writes=[wg[s_][i]])
                                dma("pool", wbr[s_][i][:], w_br[l, i].rearrange("(k p) d -> p k d", p=128)[:, :, ts(dc)], writes=[wbr[s_][i]])
                            yield
                        y_ = ysl[idx % 2]
                        for i in range(3):
                            dma("sp", y_[i][:], yT_d[i].rearrange("(h p) t -> p h t", p=128)[:, :, ts(tg, 512)], reads=[yT_b[i]], writes=[y_[i]])
                        yield
                        for i in range(3):
                            bg = tb()
                            for k in range(8):
                                mm(bg, slice(0, 512), wg[s_][i][:, k, :], hT[:, k, ts(tg, 512)], [wg[s_][i], hT], k == 0, k == 7)
                                yield
                            op("act", lambda e: e.activation(out=sg[i][:], in_=bg[:], func=AF.Sigmoid,
                                                             bias=pp[:, PP_GB + i * 8 + dc:PP_GB + i * 8 + dc + 1], scale=1.0),
                               reads=[bg, pp], writes=[sg[i]])
                            yield
                            bp = tb()
                            for k in range(4):
                                mm(bp, slice(0, 512), wbr[s_][i][:, k, :], y_[i][:, k, :], [wbr[s_][i], y_[i]], k == 0, k == 3)
                                yield
                            op("dve", lambda e: e.tensor_tensor(out=sg[i][:], in0=sg[i][:], in1=bp[:], op=ALU.mult), reads=[sg[i], bp], writes=[sg[i]])
                            yield
                        ma = macc[idx % 2]
                        op("pool", lambda e: e.tensor_tensor(out=ma[:], in0=sg[0][:], in1=sg[1][:], op=ALU.add), reads=[sg[0], sg[1]], writes=[ma])
                        yield
                        op("pool", lambda e: e.tensor_tensor(out=mergedT[:, dc, ts(tg, 512)], in0=ma[:], in1=sg[2][:], op=ALU.add),
                           reads=[ma, sg[2]], writes=[mergedT])
                        yield
                    run_window((g_m((dc * 4 + tg, dc, tg)) for dc in range(8) for tg in range(4)), 2)
                    def g_out(n):
                        xt_ = xt2[n % 2]
                        ot_ = ot2[n % 2]
                        dma("sp", xt_[:], xsrc[ts(n), :], reads=[xsrc_b] if xsrc_b else [], writes=[xt_])
                        yield
                        for half in range(2):
                            bk = tb()
                            for k in range(8):
                                mm(bk, slice(0, 512), mergedT[:, k, ts(n)], woutb[:, k, ts(half, 512)], [mergedT, woutb], k == 0, k == 7)
                                yield
                            op("dve", lambda e: e.tensor_tensor(out=ot_[:, ts(half, 512)], in0=bk[:], in1=xt_[:, ts(half, 512)], op=ALU.add),
                               reads=[bk, xt_], writes=[ot_])
                            yield
                        dma("sp", xdst[ts(n), :], ot_[:], reads=[ot_], writes=[xdst_b])
                        yield
                    run_window((g_out(n) for n in range(NT)), 2)
                P.barrier()

        P.wait_all("sp", [out_b, xs_b] + yT_b)
        for e in ("act", "dve", "pool", "pe"):
            pass
        print("ops", P.n_ops, "waits", P.n_waits, "dsems", P.n_dsem)
    return nc


def _consts():
    i = np.arange(128)
    ident = np.eye(128, dtype=np.float32)
    triu = (i[:, None] <= i[None, :]).astype(np.float32)
    ones = np.ones((128, 128), np.float32)
    maskA = np.where(i[None, :] > i[:, None], 1e4, 0.0).astype(np.float32)
    neg = np.where(i[None, :] > i[:, None], -1e30, 0.0).astype(np.float32)
    cf = np.concatenate([ident, triu, ones, maskA, neg], axis=1)
    strict = (i[None, :] < i[:, None]).astype(np.float32)
    mts = []
    for w in (2, 4, 8, 16):
        t = i[:, None]
        s = i[None, :]
        cur = np.where((s <= t) & (s > t - w), 1.0 / w, 0.0) - np.eye(128)
        prev = np.where(s > t - w + 128, 1.0 / w, 0.0)
        cnt = np.minimum(t + 1, w).astype(np.float64)
        first = np.where((s <= t) & (s > t - w), 1.0 / cnt, 0.0) - np.eye(128)
        mts += [cur.T, prev.T, first.T]
    cb = np.concatenate([ident, ones, strict] + [m.astype(np.float32) for m in mts], axis=1).astype(ml_dtypes.bfloat16)

    def rope(dim):
        inv = (np.float32(10000.0) ** (-(np.arange(0, dim, 2, dtype=np.float32) / np.float32(dim)))).astype(np.float32)
        ang = np.arange(T, dtype=np.float32)[:, None] * inv[None, :]
        c, s = np.cos(ang).astype(np.float32), np.sin(ang).astype(np.float32)
        tab = np.concatenate([c, c, -s, s], axis=1)
        return np.ascontiguousarray(tab.reshape(NT, 128, 2 * dim).transpose(1, 0, 2))
    return np.ascontiguousarray(cf), np.ascontiguousarray(cb), rope(128), rope(64)


def _pp(norm_g, gate_b, conv_w, a_log, dt_bias, dn_onorm, pool_scale, q_norm, k_norm):
    L = norm_g.shape[0]
    pp = np.zeros((L, 128, PP_W), np.float32)
    for l in range(L):
        pp[l, :, PP_CONV:PP_CONV + 48] = conv_w[l].reshape(4, 12, 128).transpose(2, 1, 0).reshape(128, 48)
        pp[l, :, PP_GB:PP_GB + 24] = gate_b[l].reshape(3, 8, 128).transpose(2, 0, 1).reshape(128, 24)
        pp[l, :, PP_PS:PP_PS + 4] = pool_scale[l].reshape(4, 128).T
        pp[l, :, PP_ALOG:PP_ALOG + 4] = a_log[l][None, :]
        pp[l, :, PP_DTB:PP_DTB + 4] = dt_bias[l][None, :]
        pp[l, :, PP_ON:PP_ON + 128] = dn_onorm[l][None, :]
        pp[l, :, PP_QN:PP_QN + 128] = q_norm[l][None, :]
        pp[l, :, PP_KN:PP_KN + 128] = k_norm[l][None, :]
        pp[l, :, PP_G:PP_G + D] = norm_g[l][None, :]
    return pp


def make_in_maps(inputs, cores):
    f = lambda k: np.ascontiguousarray(np.asarray(inputs[k], dtype=np.float32))
    cf, cb, ropeA, ropeI = _consts()
    pp = _pp(f("norm_g"), f("gate_b"), f("conv_w"), f("a_log"), f("dt_bias"), f("dn_onorm"), f("pool_scale"),
             f("q_norm"), f("k_norm"))
    x = f("x")
    shared = {"w_in": f("w_in"), "w_branch": f("w_branch"), "w_out": f("w_out"), "pool_w": f("pool_w"),
              "pp": pp, "cf": cf, "cb": cb, "ropeA": ropeA, "ropeI": ropeI}
    return [dict(shared, x=np.ascontiguousarray(x[c])) for c in cores]


def kernel(**inputs):
    nc = build(2)
    cores = list(range(8))
    res = run_bass_kernel_spmd(nc, make_in_maps(inputs, cores), core_ids=cores)
    return np.stack([np.asarray(r["out"], dtype=np.float32) for r in res.results], axis=0)
```

```python
from contextlib import ExitStack
import os
import numpy as np
import ml_dtypes
import concourse.bass as bass
import concourse.mybir as mybir
from concourse.bass_utils import run_bass_kernel_spmd

F32 = mybir.dt.float32
BF16 = mybir.dt.bfloat16
ALU = mybir.AluOpType
AF = mybir.ActivationFunctionType
AX = mybir.AxisListType

STOP = int(os.environ.get('STOP', 0))
T = 2048
D = 1024
NT = 16
INW = 8524
EPS = 1e-6
TOPK = 256
NBIS = 18
O_QA, O_KA, O_VA, O_ZA, O_BA = 0, 512, 1024, 1536, 2048
O_UB, O_ZB = 2056, 2568
O_QC, O_KC, O_VC, O_ZC, O_IQ = 3080, 3592, 4104, 4616, 5128
O_G = 5452
PP_CONV, PP_GB, PP_PS, PP_ALOG, PP_DTB, PP_ON, PP_QN, PP_KN, PP_G = 0, 48, 72, 76, 80, 84, 212, 340, 468
PP_W = 468 + 1024
CF_ID, CF_TRIU, CF_ONES, CF_MASKA, CF_NEG, CF_P2 = 0, 128, 256, 384, 512, 640
CF_W = 640 + 32
CB_ID, CB_ONES, CB_STRICT, CB_MT = 0, 128, 256, 384


class Buf:
    __slots__ = ("name", "t", "lw", "rd", "dsem", "excl")

    def __init__(self, name, t=None, excl=False):
        self.name = name
        self.t = t
        self.excl = excl
        self.lw = None
        self.rd = {}
        self.dsem = None

    def __getitem__(self, k):
        return self.t[k]


class Prog:
    COMPUTE = ("pe", "act", "dve", "pool")

    def __init__(self, nc, es, same_sync=("act", "dve", "pool")):
        self.nc = nc
        self.es = es
        self.eng = {"pe": nc.tensor, "act": nc.scalar, "dve": nc.vector, "pool": nc.gpsimd, "sp": nc.sync}
        self.sem = {}
        self.cnt = {}
        self.seen = {}
        self.same_sync = set(same_sync)
        for e in self.COMPUTE:
            self.sem[e] = es.enter_context(nc.semaphore("s_" + e))
            self.cnt[e] = 0
        self.n_dsem = 0
        self.n_ops = 0
        self.n_waits = 0

    def sb(self, name, shape, dt, stack=None):
        self.n_names = getattr(self, "n_names", 0) + 1
        name = "sb%d_%s" % (self.n_names, name)
        t = (stack or self.es).enter_context(self.nc.sbuf_tensor(name, shape, dt))
        return Buf(name, t)

    def ps(self, name, shape, dt, stack=None):
        t = (stack or self.es).enter_context(self.nc.psum_tensor(name, shape, dt))
        return Buf(name, t, excl=True)

    def dsem_for(self, buf):
        if buf.dsem is None:
            key = "d%d" % self.n_dsem
            self.n_dsem += 1
            self.sem[key] = self.es.enter_context(self.nc.semaphore("s_" + key))
            self.cnt[key] = 0
            buf.dsem = key
        return buf.dsem

    def _waits(self, issuer, selfkey, reads, writes):
        need = {}
        for b in reads:
            if b.lw is not None:
                e, n = b.lw
                if need.get(e, 0) < n:
                    need[e] = n
            if b.excl:
                for e, n in b.rd.items():
                    if e != selfkey and need.get(e, 0) < n:
                        need[e] = n
        for b in writes:
            if b.lw is not None:
                e, n = b.lw
                if need.get(e, 0) < n:
                    need[e] = n
            for e, n in b.rd.items():
                if need.get(e, 0) < n:
                    need[e] = n
        seen = self.seen.setdefault(issuer, {})
        for e, n in need.items():
            if e == selfkey and e not in self.same_sync:
                continue
            if seen.get(e, 0) < n:
                self.eng[issuer].wait_ge(self.sem[e], n)
                seen[e] = n
                self.n_waits += 1

    def _commit(self, key, n, reads, writes):
        for b in reads:
            if b.rd.get(key, 0) < n:
                b.rd[key] = n
        for b in writes:
            b.lw = (key, n)
            b.rd = {}

    def op(self, eng, emit, reads=(), writes=()):
        self._waits(eng, eng, reads, writes)
        ins = emit(self.eng[eng])
        self.cnt[eng] += 1
        ins.then_inc(self.sem[eng], 1)
        self._commit(eng, self.cnt[eng], reads, writes)
        self.n_ops += 1
        return ins

    def dma(self, queue, out, in_, reads=(), writes=(), sembuf=None, **kw):
        sbf = sembuf if sembuf is not None else writes[0]
        key = self.dsem_for(sbf)
        self._waits(queue, None, reads, writes)
        ins = self.eng[queue].dma_start(out=out, in_=in_, **kw)
        self.cnt[key] += 16
        ins.then_inc(self.sem[key], 16)
        self._commit(key, self.cnt[key], reads, writes)
        self.n_ops += 1
        return ins

    def wait_all(self, issuer, bufs):
        self._waits(issuer, issuer, bufs, ())

    def barrier(self):
        for e in self.COMPUTE:
            seen = self.seen.setdefault(e, {})
            for e2 in self.COMPUTE:
                if e2 == e:
                    continue
                n = self.cnt[e2]
                if n > 0 and seen.get(e2, 0) < n:
                    self.eng[e].wait_ge(self.sem[e2], n)
                    seen[e2] = n
            for k2, n in self.cnt.items():
                if k2 in self.COMPUTE or n == 0:
                    continue
                if seen.get(k2, 0) < n:
                    self.eng[e].wait_ge(self.sem[k2], n)
                    seen[k2] = n


def run_il(gens):
    gens = list(gens)
    while gens:
        for g in list(gens):
            try:
                next(g)
            except StopIteration:
                gens.remove(g)


def run_window(gen_iter, width):
    it = iter(gen_iter)
    active = []
    done = False
    while True:
        while not done and len(active) < width:
            try:
                active.append(next(it))
            except StopIteration:
                done = True
        if not active:
            break
        for g in list(active):
            try:
                next(g)
            except StopIteration:
                active.remove(g)


def ts(i, n=128):
    return slice(i * n, (i + 1) * n)


def build(n_layers=2, phases="NA123BCM", debug=False):
    nc = bass.Bass("TRN2", target_bir_lowering=False)
    dt_in = lambda name, shape, dt=F32: nc.dram_tensor(name, shape, dt, kind="ExternalInput").ap()
    x_in = dt_in("x", [T, D])
    w_in = dt_in("w_in", [2, D, INW])
    w_br = dt_in("w_branch", [2, 3, 512, D])
    w_out = dt_in("w_out", [2, D, D])
    pool_w = dt_in("pool_w", [2, 4, 128, 128])
    pp_d = dt_in("pp", [2, 128, PP_W])
    cf_d = dt_in("cf", [128, CF_W])
    cb_d = dt_in("cb", [128, 1920], BF16)
    ropeA_d = dt_in("ropeA", [128, 16, 256])
    ropeI_d = dt_in("ropeI", [128, 16, 128])
    out_d = nc.dram_tensor("out", [T, D], F32, kind="ExternalOutput").ap()
    skind = "ExternalOutput" if debug else "Internal"
    xs_d = nc.dram_tensor("xs", [T, D], F32, kind="Internal").ap()
    yT_d = nc.dram_tensor("yT", [3, 512, T], BF16, kind=skind).ap()

    with ExitStack() as es:
        P = Prog(nc, es)
        op, dma = P.op, P.dma
        cF = P.sb("cF", [128, CF_W], F32)
        cB = P.sb("cB", [128, 1920], BF16)
        hT = P.sb("hT", [128, 8, T], BF16)
        pp = P.sb("pp", [128, PP_W], F32)
        banks = [P.ps("bk%d" % i, [128, 512], F32) for i in range(8)]
        xs_b = Buf("xs_d")
        yT_b = [Buf("yT%d" % i) for i in range(3)]
        out_b = Buf("out_d")
        state = {"tb": 0, "pool": list(range(8))}

        def tb():
            lst = state["pool"]
            b = banks[lst[state["tb"] % len(lst)]]
            state["tb"] += 1
            return b

        def bfv(bank):
            return bank.t[:].bitcast(BF16)

        dma("sp", cF[:], cf_d, writes=[cF])
        dma("sp", cB[:], cb_d, writes=[cB])
        idF = cF[:, CF_ID:CF_ID + 128]
        triuF = cF[:, CF_TRIU:CF_TRIU + 128]
        onesF = cF[:, CF_ONES:CF_ONES + 128]
        maskA = cF[:, CF_MASKA:CF_MASKA + 128]
        negM = cF[:, CF_NEG:CF_NEG + 128]
        p2tab = cF[:, CF_P2:CF_P2 + NBIS]
        idB = cB[:, CB_ID:CB_ID + 128]
        onesB = cB[:, CB_ONES:CB_ONES + 128]
        strictB = cB[:, CB_STRICT:CB_STRICT + 128]

        def mm(bank, osl, lhsT, rhs, reads, start=True, stop=True):
            op("pe", lambda e: e.matmul(bank.t[:, osl], lhsT=lhsT, rhs=rhs, start=start, stop=stop),
               reads=reads, writes=[bank])

        def tr(bank, osl, in_, reads):
            op("pe", lambda e: e.transpose(bfv(bank)[:, osl], in_, idB), reads=list(reads) + [cB], writes=[bank])

        for l in range(n_layers):
            xsrc, xsrc_b = (x_in, None) if l == 0 else (xs_d, xs_b)
            xdst, xdst_b = (xs_d, xs_b) if l < n_layers - 1 else (out_d, out_b)
            dma("sp", pp[:], pp_d[l], writes=[pp])
            wv = w_in[l].rearrange("(k p) e -> p k e", p=128)

            with ExitStack() as ph:
                xt = [P.sb("xt%d" % i, [128, D], F32, ph) for i in range(2)]
                junk = P.sb("njunk", [128, D], BF16, ph)
                hb = [P.sb("hb%d" % i, [128, D], BF16, ph) for i in range(2)]
                ssq2 = [P.sb("ssq%d" % i, [128, 2 * NT], F32, ph) for i in range(2)]
                for q_ in ssq2:
                    op("dve", lambda e: e.memset(q_[:], 0.0), writes=[q_])
                def g_norm(n):
                    ssq = ssq2[n % 2]
                    xb_ = xt[n % 2]
                    dma("sp", xb_[:], xsrc[ts(n), :], reads=[xsrc_b] if xsrc_b else [], writes=[xb_])
                    yield
                    op("act", lambda e: e.activation(out=junk[:], in_=xb_[:], func=AF.Square,
                                                     accum_out=ssq[:, n:n + 1]), reads=[xb_], writes=[junk, ssq])
                    yield
                    op("act", lambda e: e.activation(out=ssq[:, NT + n:NT + n + 1], in_=ssq[:, n:n + 1], func=AF.Sqrt,
                                                     bias=EPS, scale=1.0 / D), reads=[ssq], writes=[ssq])
                    yield
                    op("dve", lambda e: e.reciprocal(out=ssq[:, NT + n:NT + n + 1], in_=ssq[:, NT + n:NT + n + 1]),
                       reads=[ssq], writes=[ssq])
                    yield
                    hb_ = hb[n % 2]
                    op("dve", lambda e: e.scalar_tensor_tensor(out=hb_[:], in0=xb_[:], scalar=ssq[:, NT + n:NT + n + 1],
                                                               in1=pp[:, PP_G:PP_G + D], op0=ALU.mult, op1=ALU.mult),
                       reads=[xb_, ssq, pp], writes=[hb_])
                    yield
                    bk = tb()
                    for k in range(8):
                        tr(bk, ts(k), hb_[:, ts(k)], [hb_])
                        yield
                    op("act", lambda e: e.copy(out=hT[:, :, ts(n)], in_=bfv(bk).rearrange("p (k t) -> p k t", k=8)),
                       reads=[bk], writes=[hT])
                    yield
                run_window((g_norm(n) for n in range(NT)), 2)
            P.barrier()

            def load_w(dst, c0, ncols, src=None):
                dma("pool", dst[:, :, 0:ncols], wv[:, :, c0:c0 + ncols], writes=[dst])

            if "A" in phases:
                with ExitStack() as ph:
                    qkvT = P.sb("qkvT", [128, 12, T], BF16, ph)
                    zA = P.sb("zA", [128, NT, 512], BF16, ph)
                    raw8 = P.sb("raw8", [128, NT, 8], F32, ph)
                    betaT = P.sb("betaT", [128, NT, 4], F32, ph)
                    gT = P.sb("gT", [128, NT, 4], F32, ph)
                    with ExitStack() as ph1:
                        wsl = [P.sb("wA%d" % i, [128, 8, 128], BF16, ph1) for i in range(2)]
                        cbuf = [P.sb("cbuf%d" % i, [128, 3 + T], F32, ph1) for i in range(2)]
                        acc2 = [P.sb("acc%d" % i, [128, T], F32, ph1) for i in range(2)]
                        sl2 = [P.sb("sl%d" % i, [128, T], F32, ph1) for i in range(2)]
                        sq2 = [P.sb("sq%d" % i, [128, T], BF16, ph1) for i in range(2)]
                        rn2 = [[P.sb("rn%d_%d" % (j, i), [128, 512], F32, ph1) for i in range(2)] for j in range(2)]
                        for c_ in cbuf:
                            op("pool", lambda e: e.memset(c_[:, 0:3], 0.0), writes=[c_])
                        def g_conv(cc):
                            acc, sl, sq, rn = acc2[cc % 2], sl2[cc % 2], sq2[cc % 2], rn2[cc % 2]
                            wb = wsl[cc % 2]
                            load_w(wb, cc * 128, 128)
                            yield
                            cb_ = cbuf[cc % 2]
                            for tg in range(4):
                                bk = tb()
                                for k in range(8):
                                    mm(bk, slice(0, 512), wb[:, k, :], hT[:, k, ts(tg, 512)], [wb, hT], k == 0, k == 7)
                                    yield
                                op("act", lambda e: e.copy(out=cb_[:, 3 + tg * 512:3 + (tg + 1) * 512], in_=bk[:]),
                                   reads=[bk], writes=[cb_])
                                yield
                            cw = lambda j: pp[:, PP_CONV + cc * 4 + j:PP_CONV + cc * 4 + j + 1]
                            op("pool", lambda e: e.tensor_scalar(out=acc[:], in0=cb_[:, 0:T], scalar1=cw(0), scalar2=None,
                                                                 op0=ALU.mult), reads=[cb_, pp], writes=[acc])
                            yield
                            op("dve", lambda e: e.scalar_tensor_tensor(out=acc[:], in0=cb_[:, 1:1 + T], scalar=cw(1), in1=acc[:],
                                                                        op0=ALU.mult, op1=ALU.add), reads=[cb_, pp, acc], writes=[acc])
                            yield
                            op("dve", lambda e: e.scalar_tensor_tensor(out=acc[:], in0=cb_[:, 2:2 + T], scalar=cw(2), in1=acc[:],
                                                                       op0=ALU.mult, op1=ALU.add), reads=[cb_, pp, acc], writes=[acc])
                            yield
                            op("dve", lambda e: e.scalar_tensor_tensor(out=acc[:], in0=cb_[:, 3:3 + T], scalar=cw(3), in1=acc[:],
                                                                       op0=ALU.mult, op1=ALU.add), reads=[cb_, pp, acc], writes=[acc])
                            yield
                            if cc >= 8:
                                op("act", lambda e: e.activation(out=qkvT[:, cc, :], in_=acc[:], func=AF.Silu),
                                   reads=[acc], writes=[qkvT])
                                yield
                            else:
                                op("act", lambda e: e.activation(out=sl[:], in_=acc[:], func=AF.Silu), reads=[acc], writes=[sl])
                                yield
                                op("pool", lambda e: e.tensor_tensor(out=sq[:], in0=sl[:], in1=sl[:], op=ALU.mult),
                                   reads=[sl], writes=[sq])
                                yield
                                for tg in range(4):
                                    bk = tb()
                                    mm(bk, slice(0, 512), onesB, sq[:, ts(tg, 512)], [cB, sq])
                                    yield
                                    rn_ = rn[tg % 2]
                                    op("act", lambda e: e.activation(out=rn_[:], in_=bk[:], func=AF.Sqrt, bias=EPS, scale=1.0),
                                       reads=[bk], writes=[rn_])
                                    yield
                                    op("dve", lambda e: e.reciprocal(out=rn_[:], in_=rn_[:]), reads=[rn_], writes=[rn_])
                                    yield
                                    sc = 128.0 ** -0.5 if cc < 4 else 1.0
                                    op("dve", lambda e: e.scalar_tensor_tensor(out=qkvT[:, cc, ts(tg, 512)], in0=sl[:, ts(tg, 512)],
                                                                               scalar=sc, in1=rn_[:], op0=ALU.mult, op1=ALU.mult),
                                       reads=[sl, rn_], writes=[qkvT])
                                    yield
                        run_window((g_conv(cc) for cc in range(12 if '1' in phases else 0)), 2)
                        wz = P.sb("wz", [128, 8, 520], BF16, ph1)
                        load_w(wz, O_ZA, 520)
                        zt = [P.sb("zt%d" % i, [128, 512], BF16, ph1) for i in range(2)]
                        onB = pp[:, PP_ON:PP_ON + 128].unsqueeze(1).to_broadcast([128, 4, 128])
                        def g_za(n):
                            bk = tb()
                            bk2 = tb()
                            for k in range(8):
                                mm(bk, slice(0, 512), hT[:, k, ts(n)], wz[:, k, 0:512], [wz, hT], k == 0, k == 7)
                                yield
                            for k in range(8):
                                mm(bk2, slice(0, 8), hT[:, k, ts(n)], wz[:, k, 512:520], [wz, hT], k == 0, k == 7)
                                yield
                            zt_ = zt[n % 2]
                            op("act", lambda e: e.activation(out=zt_[:], in_=bk[:], func=AF.Silu), reads=[bk], writes=[zt_])
                            yield
                            op("pool", lambda e: e.tensor_tensor(out=zA[:, n, :].rearrange("p (h d) -> p h d", h=4),
                                                                 in0=zt_[:].rearrange("p (h d) -> p h d", h=4), in1=onB, op=ALU.mult),
                               reads=[zt_, pp], writes=[zA])
                            yield
                            op("dve", lambda e: e.tensor_copy(out=raw8[:, n, :], in_=bk2[:, 0:8]), reads=[bk2], writes=[raw8])
                            yield
                        run_window((g_za(n) for n in range(NT if '2' in phases else 0)), 2)
                        op("act", lambda e: e.activation(out=betaT[:], in_=raw8[:, :, 0:4], func=AF.Sigmoid), reads=[raw8], writes=[betaT])
                        dtb = pp[:, PP_DTB:PP_DTB + 4].unsqueeze(1).to_broadcast([128, NT, 4])
                        op("dve", lambda e: e.tensor_tensor(out=gT[:], in0=raw8[:, :, 4:8], in1=dtb, op=ALU.add), reads=[raw8, pp], writes=[gT])
                        op("act", lambda e: e.activation(out=gT[:], in_=gT[:], func=AF.Exp), reads=[gT], writes=[gT])
                        op("act", lambda e: e.activation(out=gT[:], in_=gT[:], func=AF.Ln, bias=1.0, scale=1.0), reads=[gT], writes=[gT])
                        negA = P.sb("negA", [128, 4], F32, ph1)
                        op("act", lambda e: e.activation(out=negA[:], in_=pp[:, PP_ALOG:PP_ALOG + 4], func=AF.Exp), reads=[pp], writes=[negA])
                        op("dve", lambda e: e.tensor_scalar(out=negA[:], in0=negA[:], scalar1=-1.0, scalar2=None, op0=ALU.mult),
                           reads=[negA], writes=[negA])
                        op("dve", lambda e: e.tensor_tensor(out=gT[:], in0=gT[:], in1=negA[:].unsqueeze(1).to_broadcast([128, NT, 4]),
                                                            op=ALU.mult), reads=[gT, negA], writes=[gT])
                        P.barrier()
                    with ExitStack() as ph2:
                        Sf = [P.sb("Sf%d" % h, [128, 128], F32, ph2) for h in range(4)]
                        Sb = [P.sb("Sb%d" % h, [128, 128], BF16, ph2) for h in range(4)]
                        for h in range(4):
                            op("pool", lambda e: e.memset(Sf[h][:], 0.0), writes=[Sf[h]])
                            op("pool", lambda e: e.memset(Sb[h][:], 0.0), writes=[Sb[h]])
                        NB_ = 2
                        mk = lambda nm, shp, dt: [[P.sb("%s_%d_%d" % (nm, i, h), shp, dt, ph2) for h in range(4)] for i in range(NB_)]
                        g12 = [P.sb("g12_%d" % i, [128, 12], F32, ph2) for i in range(NB_)]
                        e12 = [P.sb("e12_%d" % i, [128, 12], F32, ph2) for i in range(NB_)]
                        be4 = [P.sb("be4_%d" % i, [128, 4], F32, ph2) for i in range(NB_)]
                        rhsg = mk("rhsg", [128, 128], F32)
                        dec = mk("dec", [128, 128], F32)
                        decs = mk("decs", [128, 128], F32)
                        Lb = mk("Lb", [128, 128], BF16)
                        At = mk("At", [128, 128], BF16)
                        XA = mk("XA", [128, 256], BF16)
                        kbg = mk("kbg", [128, 128], BF16)
                        kd = mk("kd", [128, 128], BF16)
                        vb = mk("vb", [128, 128], BF16)
                        XY = [mk("XY%d" % j, [128, 256], BF16) for j in range(2)]
                        Pm = [mk("Pm%d" % j, [128, 128], BF16) for j in range(2)]
                        wTs = mk("wTs", [128, 128], BF16)
                        us = mk("us", [128, 128], F32)
                        vn = mk("vn", [128, 128], BF16)
                        otmp = mk("otmp", [128, 128], F32)
                        o_all = [P.sb("o_all%d" % i, [128, 4, 128], F32, ph2) for i in range(NB_)]
                        o_sq = [P.sb("o_sq%d" % i, [128, 4, 128], F32, ph2) for i in range(NB_)]
                        r4 = [P.sb("r4_%d" % i, [128, 4], F32, ph2) for i in range(NB_)]
                        ya = [P.sb("ya%d" % i, [128, 512], BF16, ph2) for i in range(NB_)]
                        yaT = [P.sb("yaT%d" % i, [128, 4, 128], BF16, ph2) for i in range(NB_)]
                        for n in range(int(os.environ.get('A3N', NT)) if '3' in phases else 0):
                            i2 = n % NB_
                            bk = tb()
                            mm(bk, slice(0, 4), triuF, gT[:, n, :], [cF, gT])
                            mm(bk, slice(4, 8), onesF, gT[:, n, :], [cF, gT])
                            g_, e_ = g12[i2], e12[i2]
                            op("dve", lambda e: e.tensor_copy(out=g_[:, 0:4], in_=bk[:, 0:4]), reads=[bk], writes=[g_])
                            op("dve", lambda e: e.tensor_copy(out=g_[:, 8:12], in_=bk[:, 4:8]), reads=[bk], writes=[g_])
                            op("dve", lambda e: e.tensor_tensor(out=g_[:, 4:8], in0=g_[:, 8:12], in1=g_[:, 0:4], op=ALU.subtract),
                               reads=[g_], writes=[g_])
                            op("act", lambda e: e.activation(out=e_[:], in_=g_[:], func=AF.Exp), reads=[g_], writes=[e_])
                            b4 = be4[i2]
                            op("dve", lambda e: e.tensor_tensor(out=b4[:], in0=betaT[:, n, :], in1=e_[:, 0:4], op=ALU.mult),
                               reads=[betaT, e_], writes=[b4])
                            def head_gen(h, n=n, i2=i2, g_=g_, e_=e_, b4=b4):
                                kTc = qkvT[:, 4 + h, ts(n)]
                                qTc = qkvT[:, h, ts(n)]
                                vTc = qkvT[:, 8 + h, ts(n)]
                                rg, dc, dcs = rhsg[i2][h], dec[i2][h], decs[i2][h]
                                op("pool", lambda e: e.tensor_scalar(out=rg[:], in0=triuF, scalar1=gT[:, n, h:h + 1], scalar2=None,
                                                                     op0=ALU.mult), reads=[cF, gT], writes=[rg])
                                yield
                                bD = tb()
                                mm(bD, slice(0, 128), onesF, rg[:], [cF, rg], True, False)
                                yield
                                mm(bD, slice(0, 128), idF, maskA, [cF], False, True)
                                yield
                                op("act", lambda e: e.activation(out=dc[:], in_=bD[:, 0:128], func=AF.Exp, scale=-1.0,
                                                                 bias=g_[:, h:h + 1]), reads=[bD, g_], writes=[dc])
                                yield
                                op("pool", lambda e: e.tensor_tensor(out=dcs[:], in0=dc[:], in1=strictB, op=ALU.mult),
                                   reads=[dc, cB], writes=[dcs])
                                yield
                                bK = tb()
                                mm(bK, slice(0, 128), kTc, kTc, [qkvT])
                                yield
                                mm(bK, slice(128, 256), qTc, kTc, [qkvT])
                                yield
                                L_, A_ = Lb[i2][h], At[i2][h]
                                op("dve", lambda e: e.scalar_tensor_tensor(out=L_[:], in0=bK[:, 0:128], scalar=betaT[:, n, h:h + 1],
                                                                           in1=dcs[:], op0=ALU.mult, op1=ALU.mult),
                                   reads=[bK, betaT, dcs], writes=[L_])
                                yield
                                op("dve", lambda e: e.tensor_tensor(out=A_[:], in0=bK[:, 128:256], in1=dc[:], op=ALU.mult),
                                   reads=[bK, dc], writes=[A_])
                                yield
                                bT = tb()
                                tr(bT, slice(0, 128), L_[:], [L_])
                                yield
                                tr(bT, slice(128, 256), A_[:], [A_])
                                yield
                                tr(bT, slice(256, 384), kTc, [qkvT])
                                yield
                                tr(bT, slice(384, 512), vTc, [qkvT])
                                yield
                                xa = XA[i2][h]
                                op("act", lambda e: e.copy(out=xa[:], in_=bfv(bT)[:, 0:256]), reads=[bT], writes=[xa])
                                yield
                                kbg_, kd_, vb_ = kbg[i2][h], kd[i2][h], vb[i2][h]
                                op("act", lambda e: e.activation(out=kbg_[:], in_=bfv(bT)[:, 256:384], func=AF.Copy,
                                                                 scale=b4[:, h:h + 1]), reads=[bT, b4], writes=[kbg_])
                                yield
                                op("dve", lambda e: e.tensor_scalar(out=kd_[:], in0=bfv(bT)[:, 256:384], scalar1=e_[:, 4 + h:5 + h],
                                                                    scalar2=None, op0=ALU.mult), reads=[bT, e_], writes=[kd_])
                                yield
                                op("dve", lambda e: e.tensor_scalar(out=vb_[:], in0=bfv(bT)[:, 384:512], scalar1=betaT[:, n, h:h + 1],
                                                                    scalar2=None, op0=ALU.mult), reads=[bT, betaT], writes=[vb_])
                                yield
                                p_ = Pm[0][i2][h]
                                op("dve", lambda e: e.tensor_tensor(out=p_[:], in0=idB, in1=xa[:, 0:128], op=ALU.subtract),
                                   reads=[cB, xa], writes=[p_])
                                yield
                                Xc, Yc, Xb, Yb = xa[:, 0:128], L_[:], xa, L_
                                for m in range(int(os.environ.get('NEU', 6))):
                                    xy = XY[m % 2][i2][h]
                                    bN = tb()
                                    mm(bN, slice(0, 128), Yc, Xc, [Xb, Yb])
                                    yield
                                    mm(bN, slice(128, 256), Xc, Yc, [Xb, Yb])
                                    yield
                                    op("act" if m % 2 == 0 else "dve",
                                       (lambda e: e.copy(out=xy[:], in_=bN[:, 0:256])) if m % 2 == 0 else
                                       (lambda e: e.tensor_copy(out=xy[:], in_=bN[:, 0:256])), reads=[bN], writes=[xy])
                                    yield
                                    Xc, Yc, Xb, Yb = xy[:, 0:128], xy[:, 128:256], xy, xy
                                    pn = Pm[(m + 1) % 2][i2][h]
                                    bP = tb()
                                    mm(bP, slice(0, 128), idB, p_[:], [cB, p_], True, False)
                                    yield
                                    mm(bP, slice(0, 128), Yc, p_[:], [xy, p_], False, True)
                                    yield
                                    op("dve" if m % 2 == 0 else "act",
                                       (lambda e: e.tensor_copy(out=pn[:], in_=bP[:, 0:128])) if m % 2 == 0 else
                                       (lambda e: e.copy(out=pn[:], in_=bP[:, 0:128])), reads=[bP], writes=[pn])
                                    yield
                                    p_ = pn
                                TT = p_
                                bW = tb()
                                mm(bW, slice(0, 128), kbg_[:], TT[:], [kbg_, TT])
                                yield
                                mm(bW, slice(128, 256), TT[:], vb_[:], [TT, vb_])
                                yield
                                w_, u_ = wTs[i2][h], us[i2][h]
                                op("act", lambda e: e.copy(out=w_[:], in_=bW[:, 0:128]), reads=[bW], writes=[w_])
                                yield
                                op("dve", lambda e: e.tensor_copy(out=u_[:], in_=bW[:, 128:256]), reads=[bW], writes=[u_])
                                yield
                                b1 = tb()
                                mm(b1, slice(0, 128), w_[:], Sb[h][:], [w_, Sb[h]])
                                yield
                                vn_ = vn[i2][h]
                                op("dve", lambda e: e.tensor_tensor(out=vn_[:], in0=u_[:], in1=b1[:, 0:128], op=ALU.subtract),
                                   reads=[u_, b1], writes=[vn_])
                                yield
                                b2 = tb()
                                mm(b2, slice(0, 128), qTc, Sb[h][:], [qkvT, Sb[h]])
                                yield
                                mm(b2, slice(128, 256), xa[:, 128:256], vn_[:], [xa, vn_])
                                yield
                                mm(b2, slice(256, 384), kd_[:], vn_[:], [kd_, vn_])
                                yield
                                ot = otmp[i2][h]
                                op("act", lambda e: e.activation(out=ot[:], in_=b2[:, 0:128], func=AF.Copy, scale=e_[:, h:h + 1]),
                                   reads=[b2, e_], writes=[ot])
                                yield
                                oa = o_all[i2]
                                op("dve", lambda e: e.tensor_tensor(out=oa[:, h, :], in0=ot[:], in1=b2[:, 128:256], op=ALU.add),
                                   reads=[ot, b2], writes=[oa])
                                yield
                                op("dve", lambda e: e.scalar_tensor_tensor(out=Sf[h][:], in0=Sf[h][:], scalar=e_[:, 8 + h:9 + h],
                                                                           in1=b2[:, 256:384], op0=ALU.mult, op1=ALU.add),
                                   reads=[Sf[h], e_, b2], writes=[Sf[h]])
                                yield
                                op("pool", lambda e: e.tensor_copy(out=Sb[h][:], in_=Sf[h][:]), reads=[Sf[h]], writes=[Sb[h]])
                                yield
                            run_il([head_gen(h) for h in range(4)])
                            oa, osq, r_ = o_all[i2], o_sq[i2], r4[i2]
                            op("pool", lambda e: e.tensor_tensor(out=osq[:], in0=oa[:], in1=oa[:], op=ALU.mult), reads=[oa], writes=[osq])
                            op("dve", lambda e: e.tensor_reduce(out=r_[:], in_=osq[:], axis=AX.X, op=ALU.add), reads=[osq], writes=[r_])
                            op("act", lambda e: e.activation(out=r_[:], in_=r_[:], func=AF.Sqrt, bias=EPS, scale=1.0 / 128),
                               reads=[r_], writes=[r_])
                            op("dve", lambda e: e.reciprocal(out=r_[:], in_=r_[:]), reads=[r_], writes=[r_])
                            op("dve", lambda e: e.tensor_tensor(out=osq[:], in0=oa[:], in1=r_[:].unsqueeze(2).to_broadcast([128, 4, 128]),
                                                                op=ALU.mult), reads=[oa, r_, osq], writes=[osq])
                            ya_ = ya[i2]
                            op("pool", lambda e: e.tensor_tensor(out=ya_[:], in0=osq[:].rearrange("p h d -> p (h d)"), in1=zA[:, n, :],
                                                                 op=ALU.mult), reads=[osq, zA], writes=[ya_])
                            bT = tb()
                            for h in range(4):
                                tr(bT, ts(h), ya_[:, ts(h)], [ya_])
                            yT_ = yaT[i2]
                            op("act", lambda e: e.copy(out=yT_[:], in_=bfv(bT)[:, 0:512].rearrange("p (h t) -> p h t", h=4)),
                               reads=[bT], writes=[yT_])
                            dma("sp", yT_d[0].rearrange("(h p) t -> p h t", p=128)[:, :, ts(n)], yT_[:], reads=[yT_], writes=[yT_b[0]])
                        P.barrier()

            if "B" in phases:
                with ExitStack() as ph:
                    u_tok = P.sb("u_tok", [128, NT + 1, 512], BF16, ph)
                    zsT = P.sb("zsT", [128, 4, T], BF16, ph)
                    wu = P.sb("wu", [128, 8, 512], BF16, ph)
                    wzb = [P.sb("wzb%d" % i, [128, 8, 128], BF16, ph) for i in range(2)]
                    pw = P.sb("pw", [128, 4, 128], BF16, ph)
                    zst = [P.sb("zst%d" % i, [128, 512], F32, ph) for i in range(2)]
                    pooledT = [P.sb("pooledT%d" % i, [128, 4, 128], BF16, ph) for i in range(2)]
                    ybT = [P.sb("ybT%d" % i, [128, 4, 128], BF16, ph) for i in range(2)]
                    load_w(wu, O_UB, 512)
                    dma("pool", pw[:], pool_w[l].rearrange("g c d -> c g d"), writes=[pw])
                    op("pool", lambda e: e.memset(u_tok[:, 0, :], 0.0), writes=[u_tok])
                    for n in range(NT):
                        bk = tb()
                        for k in range(8):
                            mm(bk, slice(0, 512), hT[:, k, ts(n)], wu[:, k, :], [wu, hT], k == 0, k == 7)
                        op("act", lambda e: e.copy(out=u_tok[:, n + 1, :], in_=bk[:]), reads=[bk], writes=[u_tok])
                    for g in range(4):
                        w_ = wzb[g % 2]
                        load_w(w_, O_ZB + g * 128, 128)
                        for tg in range(4):
                            bk = tb()
                            for k in range(8):
                                mm(bk, slice(0, 512), w_[:, k, :], hT[:, k, ts(tg, 512)], [w_, hT], k == 0, k == 7)
                            z_ = zst[tg % 2]
                            op("act", lambda e: e.activation(out=z_[:], in_=bk[:], func=AF.Silu), reads=[bk], writes=[z_])
                            op("pool", lambda e: e.tensor_scalar(out=zsT[:, g, ts(tg, 512)], in0=z_[:], scalar1=pp[:, PP_PS + g:PP_PS + g + 1],
                                                                 scalar2=None, op0=ALU.mult), reads=[z_, pp], writes=[zsT])
                    def g_pool(n):
                        bk = tb()
                        for g in range(4):
                            mt = lambda kind: cB[:, CB_MT + (g * 3 + kind) * 128:CB_MT + (g * 3 + kind + 1) * 128]
                            if n == 0:
                                mm(bk, ts(g), u_tok[:, 1, ts(g)], mt(2), [u_tok, cB])
                                yield
                            else:
                                mm(bk, ts(g), u_tok[:, n + 1, ts(g)], mt(0), [u_tok, cB], True, False)
                                yield
                                mm(bk, ts(g), u_tok[:, n, ts(g)], mt(1), [u_tok, cB], False, True)
                                yield
                        pl = pooledT[n % 2]
                        op("act", lambda e: e.copy(out=pl[:], in_=bk[:].rearrange("p (g t) -> p g t", g=4)), reads=[bk], writes=[pl])
                        yield
                        bk2 = tb()
                        for g in range(4):
                            mm(bk2, ts(g), pw[:, g, :], pl[:, g, :], [pw, pl])
                            yield
                        yb_ = ybT[n % 2]
                        op("dve", lambda e: e.tensor_tensor(out=yb_[:], in0=bk2[:].rearrange("p (g t) -> p g t", g=4), in1=zsT[:, :, ts(n)],
                                                            op=ALU.mult), reads=[bk2, zsT], writes=[yb_])
                        yield
                        dma("sp", yT_d[1].rearrange("(h p) t -> p h t", p=128)[:, :, ts(n)], yb_[:], reads=[yb_], writes=[yT_b[1]])
                        yield
                    run_window((g_pool(n) for n in range(NT)), 2)
                P.barrier()

            if "C" in phases:
                with ExitStack() as ph:
                    kT = P.sb("kT", [128, 4, T], BF16, ph)
                    qT = P.sb("qT", [128, 4, T], BF16, ph)
                    v_tok = P.sb("v_tok", [128, NT, 512], BF16, ph)
                    zcT = P.sb("zcT", [128, 4, T], BF16, ph)
                    qiT = P.sb("qiT", [64, 4, T], BF16, ph)
                    kiT = P.sb("kiT", [64, T], BF16, ph)
                    wiT = P.sb("wiT", [128, NT, 4], F32, ph)
                    with ExitStack() as ph1:
                        ropeA = P.sb("ropeA", [128, 16, 256], F32, ph1)
                        ropeI = P.sb("ropeI", [128, 16, 128], F32, ph1)
                        dma("sp", ropeA[:], ropeA_d, writes=[ropeA])
                        dma("sp", ropeI[:], ropeI_d, writes=[ropeI])
                        wc = [P.sb("wc%d" % i, [128, 8, 512], BF16, ph1) for i in range(2)]
                        ia2 = [P.sb("ia%d" % i, [128, 5, 64], F32, ph1) for i in range(2)]
                        ib2 = [P.sb("ib%d" % i, [128, 5, 64], F32, ph1) for i in range(2)]
                        ir = [P.sb("ir%d" % i, [128, 320], BF16, ph1) for i in range(2)]
                        for wi_, c0 in enumerate((O_QC, O_KC)):
                            load_w(wc[wi_], c0, 512)
                        sqt2 = [P.sb("sqt%d" % i, [128, 4, 128], F32, ph1) for i in range(2)]
                        r4c2 = [P.sb("r4c%d" % i, [128, 4], F32, ph1) for i in range(2)]
                        qn2 = [P.sb("qn%d" % i, [128, 4, 128], F32, ph1) for i in range(2)]
                        rb2 = [P.sb("rb%d" % i, [128, 4, 128], F32, ph1) for i in range(2)]
                        qr2 = [[P.sb("qr%d_%d" % (j, i), [128, 512], BF16, ph1) for i in range(2)] for j in range(2)]
                        def g_qk(item):
                            wi_, n = item
                            w_ = wc[wi_]
                            dstT = qT if wi_ == 0 else kT
                            nB = (pp[:, PP_QN:PP_QN + 128] if wi_ == 0 else pp[:, PP_KN:PP_KN + 128]).unsqueeze(1).to_broadcast([128, 4, 128])
                            sqt, r4c, qn, rb, qr = sqt2[wi_], r4c2[wi_], qn2[wi_], rb2[wi_], qr2[wi_]
                            ra = sqt
                            bk = tb()
                            for k in range(8):
                                mm(bk, slice(0, 512), hT[:, k, ts(n)], w_[:, k, :], [w_, hT], k == 0, k == 7)
                                yield
                            bk3 = bk[:].rearrange("p (h d) -> p h d", h=4)
                            op("act", lambda e: e.activation(out=sqt[:], in_=bk3, func=AF.Square), reads=[bk], writes=[sqt])
                            yield
                            op("dve", lambda e: e.tensor_reduce(out=r4c[:], in_=sqt[:], axis=AX.X, op=ALU.add), reads=[sqt], writes=[r4c])
                            yield
                            op("act", lambda e: e.activation(out=r4c[:], in_=r4c[:], func=AF.Sqrt, bias=EPS, scale=1.0 / 128),
                               reads=[r4c], writes=[r4c])
                            yield
                            op("dve", lambda e: e.reciprocal(out=r4c[:], in_=r4c[:]), reads=[r4c], writes=[r4c])
                            yield
                            op("dve", lambda e: e.tensor_tensor(out=qn[:], in0=bk3, in1=r4c[:].unsqueeze(2).to_broadcast([128, 4, 128]),
                                                                op=ALU.mult), reads=[bk, r4c], writes=[qn])
                            yield
                            op("pool", lambda e: e.tensor_tensor(out=qn[:], in0=qn[:], in1=nB, op=ALU.mult), reads=[qn, pp], writes=[qn])
                            yield
                            cosb = ropeA[:, n, 0:128].unsqueeze(1).to_broadcast([128, 4, 128])
                            s1 = ropeA[:, n, 128:192].unsqueeze(1).to_broadcast([128, 4, 64])
                            s2 = ropeA[:, n, 192:256].unsqueeze(1).to_broadcast([128, 4, 64])
                            op("pool", lambda e: e.tensor_tensor(out=ra[:], in0=qn[:], in1=cosb, op=ALU.mult), reads=[qn, ropeA], writes=[ra])
                            yield
                            op("dve", lambda e: e.tensor_tensor(out=rb[:, :, 0:64], in0=qn[:, :, 64:128], in1=s1, op=ALU.mult),
                               reads=[qn, ropeA], writes=[rb])
                            yield
                            op("dve", lambda e: e.tensor_tensor(out=rb[:, :, 64:128], in0=qn[:, :, 0:64], in1=s2, op=ALU.mult),
                               reads=[qn, ropeA, rb], writes=[rb])
                            yield
                            qr_ = qr[n % 2]
                            op("pool", lambda e: e.tensor_tensor(out=qr_[:].rearrange("p (h d) -> p h d", h=4), in0=ra[:], in1=rb[:], op=ALU.add),
                               reads=[ra, rb], writes=[qr_])
                            yield
                            bT = tb()
                            for h in range(4):
                                tr(bT, ts(h), qr_[:, ts(h)], [qr_])
                                yield
                            op("act", lambda e: e.copy(out=dstT[:, :, ts(n)], in_=bfv(bT)[:, 0:512].rearrange("p (h t) -> p h t", h=4)),
                               reads=[bT], writes=[dstT])
                            yield
                        run_window((g_qk((wi_, n)) for n in range(NT) for wi_ in range(2)), 2)
                        w_ = wc[0]
                        load_w(w_, O_VC, 512)
                        def g_v(n):
                            bk = tb()
                            for k in range(8):
                                mm(bk, slice(0, 512), hT[:, k, ts(n)], w_[:, k, :], [w_, hT], k == 0, k == 7)
                                yield
                            op("act", lambda e: e.copy(out=v_tok[:, n, :], in_=bk[:]), reads=[bk], writes=[v_tok])
                            yield
                        run_window((g_v(n) for n in range(NT)), 2)
                        w_ = wc[1]
                        load_w(w_, O_IQ, 324)
                        def g_idx(n):
                            ia, ib = ia2[n % 2], ib2[n % 2]
                            bk = tb()
                            for k in range(8):
                                mm(bk, slice(0, 324), hT[:, k, ts(n)], w_[:, k, 0:324], [w_, hT], k == 0, k == 7)
                                yield
                            op("dve", lambda e: e.tensor_scalar(out=wiT[:, n, :], in0=bk[:, 320:324], scalar1=1.0 / 16, scalar2=None, op0=ALU.mult),
                               reads=[bk], writes=[wiT])
                            yield
                            xi = bk[:, 0:320].rearrange("p (h d) -> p h d", h=5)
                            cosb = ropeI[:, n, 0:64].unsqueeze(1).to_broadcast([128, 5, 64])
                            s1 = ropeI[:, n, 64:96].unsqueeze(1).to_broadcast([128, 5, 32])
                            s2 = ropeI[:, n, 96:128].unsqueeze(1).to_broadcast([128, 5, 32])
                            op("dve", lambda e: e.tensor_tensor(out=ia[:], in0=xi, in1=cosb, op=ALU.mult), reads=[bk, ropeI], writes=[ia])
                            yield
                            op("dve", lambda e: e.tensor_tensor(out=ib[:, :, 0:32], in0=xi[:, :, 32:64], in1=s1, op=ALU.mult),
                               reads=[bk, ropeI], writes=[ib])
                            yield
                            op("dve", lambda e: e.tensor_tensor(out=ib[:, :, 32:64], in0=xi[:, :, 0:32], in1=s2, op=ALU.mult),
                               reads=[bk, ropeI, ib], writes=[ib])
                            yield
                            ir_ = ir[n % 2]
                            op("pool", lambda e: e.tensor_tensor(out=ir_[:].rearrange("p (h d) -> p h d", h=5), in0=ia[:], in1=ib[:], op=ALU.add),
                               reads=[ia, ib], writes=[ir_])
                            yield
                            bT = tb()
                            for j in range(5):
                                op("pe", lambda e: e.transpose(bfv(bT)[0:64, ts(j)], ir_[:, j * 64:(j + 1) * 64], idB), reads=[ir_, cB], writes=[bT])
                                yield
                            op("act", lambda e: e.copy(out=qiT[:, :, ts(n)], in_=bfv(bT)[0:64, 0:512].rearrange("p (h t) -> p h t", h=4)),
                               reads=[bT], writes=[qiT])
                            yield
                            op("act", lambda e: e.copy(out=kiT[:, ts(n)], in_=bfv(bT)[0:64, 512:640]), reads=[bT], writes=[kiT])
                            yield
                        run_window((g_idx(n) for n in range(NT)), 2)
                        for g in range(4):
                            w_ = wc[g % 2]
                            load_w(w_, O_ZC + g * 128, 128)
                            for tg in range(4):
                                bk = tb()
                                for k in range(8):
                                    mm(bk, slice(0, 512), w_[:, k, 0:128], hT[:, k, ts(tg, 512)], [w_, hT], k == 0, k == 7)
                                op("act", lambda e: e.activation(out=zcT[:, g, ts(tg, 512)], in_=bk[:], func=AF.Silu), reads=[bk], writes=[zcT])
                        P.barrier()
                    with ExitStack() as ph2:
                        score = [P.sb("score%d" % i, [128, T], F32, ph2) for i in range(2)]
                        relu_t = [P.sb("relu%d" % i, [128, 512], F32, ph2) for i in range(4)]
                        maskts = [P.sb("maskts%d" % i, [128, T], BF16, ph2) for i in range(2)]
                        maskT = [P.sb("maskT%d" % i, [128, NT, 128], BF16, ph2) for i in range(4)]
                        eT = [P.sb("eT%d" % i, [128, 512], BF16, ph2) for i in range(4)]
                        pT = [P.sb("pT%d" % i, [128, 512], BF16, ph2) for i in range(4)]
                        bis = [P.sb("bis%d" % i, [128, 8], F32, ph2) for i in range(2)]
                        negw = [P.sb("negw%d" % i, [128, NBIS], F32, ph2) for i in range(2)]
                        halfw = [P.sb("halfw%d" % i, [128, NBIS], F32, ph2) for i in range(2)]
                        rden = [P.sb("rden%d" % i, [128, 512], F32, ph2) for i in range(2)]
                        otc = [P.sb("otc%d" % i, [128, 512], F32, ph2) for i in range(2)]
                        ycT_t = [P.sb("ycT%d" % i, [128, 4, 128], BF16, ph2) for i in range(2)]
                        state["pool"] = list(range(4))
                        cst = {"e": 0, "r": 0}

                        def sel_gen(qb):
                            nk = 128 * (qb + 1)
                            sc = score[qb % 2]
                            for c0 in range(0, nk, 512):
                                c1 = min(nk, c0 + 512)
                                w = c1 - c0
                                for h in range(4):
                                    bk = tb()
                                    mm(bk, slice(0, w), qiT[0:64, h, ts(qb)], kiT[0:64, c0:c1], [qiT, kiT])
                                    yield
                                    rl = relu_t[cst["r"] % 4]
                                    cst["r"] += 1
                                    op("act", lambda e: e.activation(out=rl[:, 0:w], in_=bk[:, 0:w], func=AF.Relu), reads=[bk], writes=[rl])
                                    yield
                                    if h == 0:
                                        op("dve", lambda e: e.tensor_scalar(out=sc[:, c0:c1], in0=rl[:, 0:w], scalar1=wiT[:, qb, 0:1], scalar2=None,
                                                                            op0=ALU.mult), reads=[rl, wiT], writes=[sc])
                                    else:
                                        op("dve", lambda e: e.scalar_tensor_tensor(out=sc[:, c0:c1], in0=rl[:, 0:w], scalar=wiT[:, qb, h:h + 1],
                                                                                   in1=sc[:, c0:c1], op0=ALU.mult, op1=ALU.add),
                                           reads=[rl, wiT, sc], writes=[sc])
                                    yield
                            op("pool", lambda e: e.tensor_tensor(out=sc[:, qb * 128:nk], in0=sc[:, qb * 128:nk], in1=negM, op=ALU.add),
                               reads=[sc, cF], writes=[sc])
                            yield
                            mts = maskts[qb % 2]
                            if qb >= 2:
                                b_ = bis[qb % 2]
                                nw, hw = negw[qb % 2], halfw[qb % 2]
                                op("dve", lambda e: e.tensor_reduce(out=b_[:, 1:2], in_=sc[:, 0:nk], axis=AX.X, op=ALU.max), reads=[sc], writes=[b_])
                                yield
                                op("dve", lambda e: e.tensor_reduce(out=b_[:, 0:1], in_=sc[:, 0:nk - 128], axis=AX.X, op=ALU.min), reads=[sc, b_], writes=[b_])
                                yield
                                op("dve", lambda e: e.tensor_tensor(out=b_[:, 1:2], in0=b_[:, 1:2], in1=b_[:, 0:1], op=ALU.subtract), reads=[b_], writes=[b_])
                                yield
                                op("dve", lambda e: e.tensor_scalar(out=nw[:], in0=p2tab, scalar1=b_[:, 1:2], scalar2=-1.0, op0=ALU.mult, op1=ALU.mult),
                                   reads=[cF, b_], writes=[nw])
                                yield
                                op("dve", lambda e: e.tensor_scalar(out=hw[:], in0=p2tab, scalar1=b_[:, 1:2], scalar2=0.5, op0=ALU.mult, op1=ALU.mult),
                                   reads=[cF, b_], writes=[hw])
                                yield
                                op("dve", lambda e: e.tensor_scalar(out=b_[:, 2:3], in0=b_[:, 0:1], scalar1=nw[:, 0:1], scalar2=-1.0,
                                                                    op0=ALU.subtract, op1=ALU.mult), reads=[b_, nw], writes=[b_])
                                yield
                                for it in range(NBIS):
                                    op("act", lambda e: e.activation(out=mts[:, 0:nk], in_=sc[:, 0:nk], func=AF.Sign, bias=b_[:, 2:3], scale=1.0,
                                                                     accum_out=b_[:, 3:4]), reads=[sc, b_], writes=[mts, b_])
                                    yield
                                    op("dve", lambda e: e.scalar_tensor_tensor(out=b_[:, 4:5], in0=b_[:, 3:4], scalar=float(2 * TOPK - nk), in1=nw[:, it:it + 1],
                                                                               op0=ALU.is_ge, op1=ALU.mult), reads=[b_, nw], writes=[b_])
                                    yield
                                    op("dve", lambda e: e.scalar_tensor_tensor(out=b_[:, 2:3], in0=b_[:, 4:5], scalar=hw[:, it:it + 1], in1=b_[:, 2:3],
                                                                               op0=ALU.add, op1=ALU.add), reads=[b_, hw], writes=[b_])
                                    yield
                                op("dve", lambda e: e.tensor_scalar(out=b_[:, 5:6], in0=b_[:, 2:3], scalar1=-1.0, scalar2=hw[:, NBIS - 1:NBIS],
                                                                    op0=ALU.mult, op1=ALU.subtract), reads=[b_, hw], writes=[b_])
                                yield
                                thr, thr_r = b_[:, 5:6], [b_]
                            else:
                                thr, thr_r = -1e29, []
                            op("dve", lambda e: e.tensor_scalar(out=mts[:, 0:nk], in0=sc[:, 0:nk], scalar1=thr, scalar2=None, op0=ALU.is_ge),
                               reads=[sc] + thr_r, writes=[mts])
                            yield
                            mT = maskT[qb % 4]
                            for kb0 in range(0, qb + 1, 4):
                                nb = min(4, qb + 1 - kb0)
                                bT = tb()
                                for j in range(nb):
                                    tr(bT, ts(j), mts[:, ts(kb0 + j)], [mts])
                                yield
                                op("act", lambda e: e.copy(out=mT[:, kb0:kb0 + nb, :], in_=bfv(bT)[:, 0:nb * 128].rearrange("p (g t) -> p g t", g=nb)),
                                   reads=[bT], writes=[mT])
                                yield

                        def att_gen(qb):
                            mT = maskT[qb % 4]
                            Ob, Dn = banks[4 + 2 * (qb % 2)], banks[5 + 2 * (qb % 2)]
                            for kb in range(qb + 1):
                                bk = tb()
                                for h in range(4):
                                    mm(bk, ts(h), kT[:, h, ts(kb)], qT[:, h, ts(qb)], [kT, qT])
                                yield
                                e_ = eT[cst["e"] % 4]
                                p_ = pT[cst["e"] % 4]
                                cst["e"] += 1
                                op("act", lambda e: e.activation(out=e_[:], in_=bk[:], func=AF.Exp, scale=128.0 ** -0.5), reads=[bk], writes=[e_])
                                yield
                                op("pool", lambda e: e.tensor_tensor(out=p_[:].rearrange("p (h t) -> p h t", h=4),
                                                                     in0=e_[:].rearrange("p (h t) -> p h t", h=4),
                                                                     in1=mT[:, kb, :].unsqueeze(1).to_broadcast([128, 4, 128]), op=ALU.mult),
                                   reads=[e_, mT], writes=[p_])
                                yield
                                for h in range(4):
                                    mm(Ob, ts(h), v_tok[:, kb, ts(h)], p_[:, ts(h)], [v_tok, p_], kb == 0 and h == 0, kb == qb)
                                mm(Dn, slice(0, 512), onesB, p_[:], [cB, p_], kb == 0, kb == qb)
                                yield
                            rd, ot_ = rden[qb % 2], otc[qb % 2]
                            op("dve", lambda e: e.reciprocal(out=rd[:], in_=Dn[:]), reads=[Dn], writes=[rd])
                            yield
                            op("dve", lambda e: e.tensor_tensor(out=ot_[:], in0=Ob[:], in1=rd[:], op=ALU.mult), reads=[Ob, rd], writes=[ot_])
                            yield
                            yc_ = ycT_t[qb % 2]
                            op("pool", lambda e: e.tensor_tensor(out=yc_[:], in0=ot_[:].rearrange("p (h t) -> p h t", h=4), in1=zcT[:, :, ts(qb)],
                                                                 op=ALU.mult), reads=[ot_, zcT], writes=[yc_])
                            yield
                            dma("sp", yT_d[2].rearrange("(h p) t -> p h t", p=128)[:, :, ts(qb)], yc_[:], reads=[yc_], writes=[yT_b[2]])
                            yield

                        for r in range(NT // 2 + 1):
                            gens = []
                            if r >= 1:
                                gens += [att_gen(2 * r - 2), att_gen(2 * r - 1)]
                            if r < NT // 2:
                                gens += [sel_gen(2 * r), sel_gen(2 * r + 1)]
                            run_il(gens)
                        state["pool"] = list(range(8))
                P.barrier()

            if "M" in phases:
                with ExitStack() as ph:
                    mergedT = P.sb("mergedT", [128, 8, T], BF16, ph)
                    woutb = P.sb("woutb", [128, 8, D], BF16, ph)
                    dma("pool", woutb[:], w_out[l].rearrange("(k p) e -> p k e", p=128), writes=[woutb])
                    wg = [[P.sb("wg%d_%d" % (s_, i), [128, 8, 128], BF16, ph) for i in range(3)] for s_ in range(2)]
                    wbr = [[P.sb("wbr%d_%d" % (s_, i), [128, 4, 128], BF16, ph) for i in range(3)] for s_ in range(2)]
                    ysl = [[P.sb("ysl%d_%d" % (s_, i), [128, 4, 512], BF16, ph) for i in range(3)] for s_ in range(2)]
                    macc = [P.sb("macc%d" % i, [128, 512], F32, ph) for i in range(2)]
                    xt2 = [P.sb("xt2_%d" % i, [128, D], F32, ph) for i in range(2)]
                    ot2 = [P.sb("ot2_%d" % i, [128, D], F32, ph) for i in range(2)]
                    sg2 = [[P.sb("sgx%d_%d" % (j, i), [128, 512], F32, ph) for i in range(3)] for j in range(2)]
                    def g_m(item):
                        idx, dc, tg = item
                        s_ = dc % 2
                        sg = sg2[idx % 2]
                        if tg == 0:
                            for i in range(3):
                                dma("pool", wg[s_][i][:], wv[:, :, O_G + i * 1024 + dc * 128:O_G + i * 1024 + (dc + 1) * 128], writes=[wg[s_][i]])
                                dma("pool", wbr[s_][i][:], w_br[l, i].rearrange("(k p) d -> p k d", p=128)[:, :, ts(dc)], writes=[wbr[s_][i]])
                            yield
                        y_ = ysl[idx % 2]
                        for i in range(3):
                            dma("sp", y_[i][:], yT_d[i].rearrange("(h p) t -> p h t", p=128)[:, :, ts(tg, 512)], reads=[yT_b[i]], writes=[y_[i]])
                        yield
                        for i in range(3):
                            bg = tb()
                            for k in range(8):
                                mm(bg, slice(0, 512), wg[s_][i][:, k, :], hT[:, k, ts(tg, 512)], [wg[s_][i], hT], k == 0, k == 7)
                                yield
                            op("act", lambda e: e.activation(out=sg[i][:], in_=bg[:], func=AF.Sigmoid,
                                                             bias=pp[:, PP_GB + i * 8 + dc:PP_GB + i * 8 + dc + 1], scale=1.0),
                               reads=[bg, pp], writes=[sg[i]])
                            yield
                            bp = tb()
                            for k in range(4):
                                mm(bp, slice(0, 512), wbr[s_][i][:, k, :], y_[i][:, k, :], [wbr[s_][i], y_[i]], k == 0, k == 3)
                                yield
                            op("dve", lambda e: e.tensor_tensor(out=sg[i][:], in0=sg[i][:], in1=bp[:], op=ALU.mult), reads=[sg[i], bp], writes=[sg[i]])
                            yield
                        ma = macc[idx % 2]
                        op("pool", lambda e: e.tensor_tensor(out=ma[:], in0=sg[0][:], in1=sg[1][:], op=ALU.add), reads=[sg[0], sg[1]], writes=[ma])
                        yield
                        op("pool", lambda e: e.tensor_tensor(out=mergedT[:, dc, ts(tg, 512)], in0=ma[:], in1=sg[2][:], op=ALU.add),
                           reads=[ma, sg[2]], writes=[mergedT])
                        yield
                    run_window((g_m((dc * 4 + tg, dc, tg)) for dc in range(8) for tg in range(4)), 2)
                    def g_out(n):
                        xt_ = xt2[n % 2]
                        ot_ = ot2[n % 2]
                        dma("sp", xt_[:], xsrc[ts(n), :], reads=[xsrc_b] if xsrc_b else [], writes=[xt_])
                        yield
                        for half in range(2):
                            bk = tb()
                            for k in range(8):
                                mm(bk, slice(0, 512), mergedT[:, k, ts(n)], woutb[:, k, ts(half, 512)], [mergedT, woutb], k == 0, k == 7)
                                yield
                            op("dve", lambda e: e.tensor_tensor(out=ot_[:, ts(half, 512)], in0=bk[:], in1=xt_[:, ts(half, 512)], op=ALU.add),
                               reads=[bk, xt_], writes=[ot_])
                            yield
                        dma("sp", xdst[ts(n), :], ot_[:], reads=[ot_], writes=[xdst_b])
                        yield
                    run_window((g_out(n) for n in range(NT)), 2)
                P.barrier()

        P.wait_all("sp", [out_b, xs_b] + yT_b)
        for e in ("act", "dve", "pool", "pe"):
            pass
        print("ops", P.n_ops, "waits", P.n_waits, "dsems", P.n_dsem)
    return nc


def _consts():
    i = np.arange(128)
    ident = np.eye(128, dtype=np.float32)
    triu = (i[:, None] <= i[None, :]).astype(np.float32)
    ones = np.ones((128, 128), np.float32)
    maskA = np.where(i[None, :] > i[:, None], 1e4, 0.0).astype(np.float32)
    neg = np.where(i[None, :] > i[:, None], -1e30, 0.0).astype(np.float32)
    p2 = np.tile((2.0 ** -(np.arange(32) + 1.0)).astype(np.float32)[None, :], (128, 1))
    cf = np.concatenate([ident, triu, ones, maskA, neg, p2], axis=1)
    strict = (i[None, :] < i[:, None]).astype(np.float32)
    mts = []
    for w in (2, 4, 8, 16):
        t = i[:, None]
        s = i[None, :]
        cur = np.where((s <= t) & (s > t - w), 1.0 / w, 0.0) - np.eye(128)
        prev = np.where(s > t - w + 128, 1.0 / w, 0.0)
        cnt = np.minimum(t + 1, w).astype(np.float64)
        first = np.where((s <= t) & (s > t - w), 1.0 / cnt, 0.0) - np.eye(128)
        mts += [cur.T, prev.T, first.T]
    cb = np.concatenate([ident, ones, strict] + [m.astype(np.float32) for m in mts], axis=1).astype(ml_dtypes.bfloat16)

    def rope(dim):
        inv = (np.float32(10000.0) ** (-(np.arange(0, dim, 2, dtype=np.float32) / np.float32(dim)))).astype(np.float32)
        ang = np.arange(T, dtype=np.float32)[:, None] * inv[None, :]
        c, s = np.cos(ang).astype(np.float32), np.sin(ang).astype(np.float32)
        tab = np.concatenate([c, c, -s, s], axis=1)
        return np.ascontiguousarray(tab.reshape(NT, 128, 2 * dim).transpose(1, 0, 2))
    return np.ascontiguousarray(cf), np.ascontiguousarray(cb), rope(128), rope(64)


def _pp(norm_g, gate_b, conv_w, a_log, dt_bias, dn_onorm, pool_scale, q_norm, k_norm):
    L = norm_g.shape[0]
    pp = np.zeros((L, 128, PP_W), np.float32)
    for l in range(L):
        pp[l, :, PP_CONV:PP_CONV + 48] = conv_w[l].reshape(4, 12, 128).transpose(2, 1, 0).reshape(128, 48)
        pp[l, :, PP_GB:PP_GB + 24] = gate_b[l].reshape(3, 8, 128).transpose(2, 0, 1).reshape(128, 24)
        pp[l, :, PP_PS:PP_PS + 4] = pool_scale[l].reshape(4, 128).T
        pp[l, :, PP_ALOG:PP_ALOG + 4] = a_log[l][None, :]
        pp[l, :, PP_DTB:PP_DTB + 4] = dt_bias[l][None, :]
        pp[l, :, PP_ON:PP_ON + 128] = dn_onorm[l][None, :]
        pp[l, :, PP_QN:PP_QN + 128] = q_norm[l][None, :]
        pp[l, :, PP_KN:PP_KN + 128] = k_norm[l][None, :]
        pp[l, :, PP_G:PP_G + D] = norm_g[l][None, :]
    return pp


def make_in_maps(inputs, cores):
    f = lambda k: np.ascontiguousarray(np.asarray(inputs[k], dtype=np.float32))
    cf, cb, ropeA, ropeI = _consts()
    pp = _pp(f("norm_g"), f("gate_b"), f("conv_w"), f("a_log"), f("dt_bias"), f("dn_onorm"), f("pool_scale"),
             f("q_norm"), f("k_norm"))
    x = f("x")
    shared = {"w_in": f("w_in"), "w_branch": f("w_branch"), "w_out": f("w_out"), "pool_w": f("pool_w"),
              "pp": pp, "cf": cf, "cb": cb, "ropeA": ropeA, "ropeI": ropeI}
    return [dict(shared, x=np.ascontiguousarray(x[c])) for c in cores]


def kernel(**inputs):
    nc = build(2)
    cores = list(range(8))
    res = run_bass_kernel_spmd(nc, make_in_maps(inputs, cores), core_ids=cores)
    return np.stack([np.asarray(r["out"], dtype=np.float32) for r in res.results], axis=0)
```
